# Optimizing a Trainium2 kernel written in Bass

```python
import jax, jax.numpy as jnp
from jax import lax
import numpy as np

D_MODEL = 1024
BATCH = 32
SEQ = 2048
DEPTH = 1

N_META = 16
MIX_WIDTH = D_MODEL
CONV_DIM = MIX_WIDTH // 2
ATTN_DIM = MIX_WIDTH - CONV_DIM
HEAD_DIM = 64
N_HEADS = ATTN_DIM // HEAD_DIM
N_CONV_GROUPS = CONV_DIM // HEAD_DIM
CONV_K = 3
D_FF = ((8 * D_MODEL // 3 + 255) // 256) * 256
Q_BLOCK = 128
IN_DIM = 3 * CONV_DIM + 3 * ATTN_DIM + N_HEADS
EPS = 1e-6

kernel_name = "hymba_conv_fox_macaron_layer"


def rms_norm(x, g):
    xf = x.astype(jnp.float32)
    y = xf * lax.rsqrt(jnp.mean(xf * xf, axis=-1, keepdims=True) + EPS)
    return (y * g.astype(jnp.float32)).astype(x.dtype)


def group_rms_norm(x, g, n_groups):
    b, l, c = x.shape
    xg = x.astype(jnp.float32).reshape(b, l, n_groups, c // n_groups)
    xg = xg * lax.rsqrt(jnp.mean(xg * xg, axis=-1, keepdims=True) + EPS)
    return (xg.reshape(b, l, c) * g.astype(jnp.float32)).astype(x.dtype)


def swiglu_ffn(h, w_gu, w_down):
    gate, up = jnp.split(h @ w_gu, 2, axis=-1)
    return (jax.nn.silu(gate) * up) @ w_down


def short_conv_mixer(b_gate, c_gate, hc, conv_w):
    u = c_gate * hc
    y = lax.conv_general_dilated(
        u, conv_w.astype(u.dtype)[:, None, :], window_strides=(1,),
        padding=[(CONV_K - 1, 0)], dimension_numbers=('NWC', 'WIO', 'NWC'),
        feature_group_count=CONV_DIM)
    return b_gate * y


def fox_block(q_blk, fq_blk, q_pos, k, v, fk, k_pos):
    s = jnp.einsum('bhqd,bhkd->bhqk', q_blk, k, preferred_element_type=jnp.float32) * (HEAD_DIM ** -0.5)
    s = s + fq_blk[..., :, None] - fk[..., None, :]
    s = jnp.where(k_pos[None, :] <= q_pos[:, None], s, -jnp.inf)
    p = jax.nn.softmax(s, axis=-1)
    return jnp.einsum('bhqk,bhkd->bhqd', p.astype(v.dtype), v)


def forgetting_attention(q, k, v, fg_logit, b_f):
    bsz, L = q.shape[0], q.shape[1]
    log_f = jax.nn.log_sigmoid(fg_logit.astype(jnp.float32) + b_f.astype(jnp.float32))
    F = jnp.cumsum(log_f, axis=1).transpose(0, 2, 1)
    q, k, v = (t.transpose(0, 2, 1, 3) for t in (q, k, v))
    pos = jnp.arange(L)
    o_meta = fox_block(q[:, :, :N_META], F[:, :, :N_META], pos[:N_META],
                       k[:, :, :N_META], v[:, :, :N_META], F[:, :, :N_META], pos[:N_META])
    n_blk = (L - N_META) // Q_BLOCK
    qr = q[:, :, N_META:].reshape(bsz, N_HEADS, n_blk, Q_BLOCK, HEAD_DIM).transpose(2, 0, 1, 3, 4)
    fr = F[:, :, N_META:].reshape(bsz, N_HEADS, n_blk, Q_BLOCK).transpose(2, 0, 1, 3)
    pr = pos[N_META:].reshape(n_blk, Q_BLOCK)
    o_real = lax.map(lambda a: fox_block(a[0], a[1], a[2], k, v, F, pos), (qr, fr, pr))
    o_real = o_real.transpose(1, 2, 0, 3, 4).reshape(bsz, N_HEADS, L - N_META, HEAD_DIM)
    o = jnp.concatenate([o_meta, o_real], axis=2)
    return o.transpose(0, 2, 1, 3).reshape(bsz, L, ATTN_DIM)


def hybrid_mixer(h, w_in, conv_w, b_f, g_conv, g_attn, w_out):
    bsz, L, _ = h.shape
    proj = h @ w_in
    c0 = 3 * CONV_DIM
    b_gate, c_gate, hc, q, k, v, fg = jnp.split(
        proj, [CONV_DIM, 2 * CONV_DIM, c0, c0 + ATTN_DIM, c0 + 2 * ATTN_DIM, c0 + 3 * ATTN_DIM], axis=-1)
    y_conv = short_conv_mixer(b_gate, c_gate, hc, conv_w)
    hs = (bsz, L, N_HEADS, HEAD_DIM)
    y_attn = forgetting_attention(q.reshape(hs), k.reshape(hs), v.reshape(hs), fg, b_f)
    y = jnp.concatenate([group_rms_norm(y_conv, g_conv, N_CONV_GROUPS),
                         group_rms_norm(y_attn, g_attn, N_HEADS)], axis=-1)
    return y @ w_out


def setup_inputs(seed: int = 0) -> dict:
    key = jax.random.key(seed)
    ks = jax.random.split(key, 20)
    nrm = lambda k, shape, scale: jax.random.normal(k, shape, jnp.float32) * scale
    gain = lambda k, shape: 1.0 + 0.02 * jax.random.normal(k, shape, jnp.float32)
    return {
        'x': nrm(ks[0], (BATCH, SEQ, D_MODEL), 1.0),
        'meta_tokens': nrm(ks[1], (N_META, D_MODEL), 1.0),
        'ffn1_norm': gain(ks[2], (DEPTH, D_MODEL)),
        'ffn1_w_gu': nrm(ks[3], (DEPTH, D_MODEL, 2 * D_FF), D_MODEL ** -0.5),
        'ffn1_w_down': nrm(ks[4], (DEPTH, D_FF, D_MODEL), D_FF ** -0.5),
        'mix_norm': gain(ks[5], (DEPTH, D_MODEL)),
        'w_in': nrm(ks[6], (DEPTH, D_MODEL, IN_DIM), D_MODEL ** -0.5),
        'conv_w': nrm(ks[7], (DEPTH, CONV_K, CONV_DIM), CONV_K ** -0.5),
        'b_f': jnp.linspace(1.0, 6.0, N_HEADS, dtype=jnp.float32)[None, :] + nrm(ks[8], (DEPTH, N_HEADS), 0.1),
        'out_norm_conv': gain(ks[9], (DEPTH, CONV_DIM)),
        'out_norm_attn': gain(ks[10], (DEPTH, ATTN_DIM)),
        'w_out': nrm(ks[11], (DEPTH, MIX_WIDTH, D_MODEL), MIX_WIDTH ** -0.5),
        'ffn2_norm': gain(ks[12], (DEPTH, D_MODEL)),
        'ffn2_w_gu': nrm(ks[13], (DEPTH, D_MODEL, 2 * D_FF), D_MODEL ** -0.5),
        'ffn2_w_down': nrm(ks[14], (DEPTH, D_FF, D_MODEL), D_FF ** -0.5),
        'final_norm': gain(ks[15], (D_MODEL,)),
    }


def reference(x, meta_tokens, ffn1_norm, ffn1_w_gu, ffn1_w_down, mix_norm, w_in, conv_w, b_f,
              out_norm_conv, out_norm_attn, w_out, ffn2_norm, ffn2_w_gu, ffn2_w_down, final_norm):
    bsz = x.shape[0]
    meta = jnp.broadcast_to(meta_tokens.astype(x.dtype)[None], (bsz, N_META, D_MODEL))
    h = jnp.concatenate([meta, x], axis=1)
    for l in range(DEPTH):
        h = h + 0.5 * swiglu_ffn(rms_norm(h, ffn1_norm[l]), ffn1_w_gu[l], ffn1_w_down[l])
        h = h + hybrid_mixer(rms_norm(h, mix_norm[l]), w_in[l], conv_w[l], b_f[l],
                             out_norm_conv[l], out_norm_attn[l], w_out[l])
        h = h + 0.5 * swiglu_ffn(rms_norm(h, ffn2_norm[l]), ffn2_w_gu[l], ffn2_w_down[l])
    h = h[:, N_META:]
    return rms_norm(h, final_norm)
```

```python
import numpy as np
import concourse.bass as bass
import concourse.mybir as mybir
from concourse.bass_utils import run_bass_kernel_spmd

F32 = mybir.dt.float32
BF16 = mybir.dt.bfloat16
AF = mybir.ActivationFunctionType
ALU = mybir.AluOpType
AX = mybir.AxisListType

D = 1024
SEQ = 2048
NMETA = 16
L = SEQ + NMETA
NT = 17
DFF = 2816
NCH = DFF // 128
EPS = 1e-6
N_CORES = 8
SEQ_PER_CORE = 4
FF_GROUPS = [(0, 4), (4, 4), (8, 4), (12, 4), (16, 4), (20, 2)]
SLOT_ELEMS = 4096
N_SLOTS = 5
MASK_NEG = -30000.0


def tile_rows(j):
    return NMETA if j == 0 else 128


def tile_col0(j):
    return 0 if j == 0 else NMETA + (j - 1) * 128


TOK_GROUPS = [[0], [1, 2, 3, 4], [5, 6, 7, 8], [9, 10, 11, 12], [13, 14, 15, 16]]


def group_cols(g):
    tl = TOK_GROUPS[g]
    c0 = tile_col0(tl[0])
    c1 = tile_col0(tl[-1]) + tile_rows(tl[-1])
    return c0, c1


C_IDENT = 0
C_NEGTRI = 128
C_NEGONES = 256
C_MASK = 384
C_BLK = 512
C_G1 = 640
C_GM = 648
C_G2 = 656
C_GFIN = 664
C_CONVW = C_GFIN + 1024
C_GCONV = C_CONVW + 12
C_GATTN = C_GCONV + 4
C_BF = C_GATTN + 4
C_TOTAL = C_BF + 8


def make_consts(inp):
    c = np.zeros((128, C_TOTAL), np.float32)
    idx = np.arange(128)
    c[:, C_IDENT:C_IDENT + 128] = np.eye(128, dtype=np.float32)
    c[:, C_NEGTRI:C_NEGTRI + 128] = np.where(idx[:, None] <= idx[None, :], -1.0, 0.0)
    c[:, C_NEGONES:C_NEGONES + 128] = -1.0
    c[:, C_MASK:C_MASK + 128] = np.where(idx[:, None] <= idx[None, :], 0.0, MASK_NEG)
    c[:, C_BLK:C_BLK + 128] = np.where((idx[:, None] // 64) == (idx[None, :] // 64), 1.0 / 64, 0.0)
    c[:, C_G1:C_G1 + 8] = inp['ffn1_norm'].reshape(8, 128).T
    c[:, C_GM:C_GM + 8] = inp['mix_norm'].reshape(8, 128).T
    c[:, C_G2:C_G2 + 8] = inp['ffn2_norm'].reshape(8, 128).T
    c[:, C_GFIN:C_GFIN + 1024] = inp['final_norm'].reshape(1, 1024)
    cw = inp['conv_w'].reshape(3, 4, 128)
    c[:, C_CONVW:C_CONVW + 12] = cw.transpose(2, 1, 0).reshape(128, 12)
    c[:, C_GCONV:C_GCONV + 4] = inp['out_norm_conv'].reshape(4, 128).T
    c[:, C_GATTN:C_GATTN + 4] = inp['out_norm_attn'].reshape(4, 128).T
    c[:, C_BF:C_BF + 8] = inp['b_f'].reshape(1, 8)
    return np.ascontiguousarray(c)


def ffn_layout(w_gu, w_down):
    w_gu = w_gu.reshape(D, 2 * DFF)
    w_down = w_down.reshape(DFF, D)
    gu = w_gu.reshape(8, 128, 2 * DFF).transpose(1, 0, 2)
    dn = w_down.reshape(NCH, 128, D).transpose(1, 0, 2)
    pieces = []
    for (c0, n) in FF_GROUPS:
        pieces.append(gu[:, :, c0 * 128:(c0 + n) * 128].reshape(128, -1))
        pieces.append(gu[:, :, DFF + c0 * 128:DFF + (c0 + n) * 128].reshape(128, -1))
        pieces.append(dn[:, c0:c0 + n, :].reshape(128, -1))
    return np.ascontiguousarray(np.concatenate(pieces, axis=1))


def ffn_piece_offsets():
    offs = []
    o = 0
    for (c0, n) in FF_GROUPS:
        g = (o, 8 * n * 128); o += 8 * n * 128
        u = (o, 8 * n * 128); o += 8 * n * 128
        d = (o, n * 1024); o += n * 1024
        offs.append((g, u, d))
    return offs, o


MIX_FG = 64
MIX_UNIT = 8 * 3 * 128 + 1024


def mix_layout(w_in, w_out):
    w_in = w_in.reshape(D, 3080)
    w_out = w_out.reshape(D, D)
    wi = w_in.reshape(8, 128, 3080).transpose(1, 0, 2)
    wo = w_out.reshape(8, 128, D).transpose(1, 0, 2)
    pieces = [wi[:, :, 3072:3080].reshape(128, -1)]
    for cc in range(4):
        blk = np.stack([wi[:, :, cc * 128:(cc + 1) * 128],
                        wi[:, :, 512 + cc * 128:512 + (cc + 1) * 128],
                        wi[:, :, 1024 + cc * 128:1024 + (cc + 1) * 128]], axis=2)
        pieces.append(blk.reshape(128, -1))
        pieces.append(wo[:, cc, :])
    for p in range(4):
        blk = np.stack([wi[:, :, 1536 + p * 128:1536 + (p + 1) * 128],
                        wi[:, :, 2048 + p * 128:2048 + (p + 1) * 128],
                        wi[:, :, 2560 + p * 128:2560 + (p + 1) * 128]], axis=2)
        pieces.append(blk.reshape(128, -1))
        pieces.append(wo[:, 4 + p, :])
    return np.ascontiguousarray(np.concatenate(pieces, axis=1))


MIX_TOTAL = MIX_FG + 8 * MIX_UNIT


COMPUTE = ('pe', 'act', 'dve')


class Op:
    __slots__ = ('eng', 'fn', 'deps', 'dma', 'idx', 'sig', 'sigval', 'sem', 'prewait')

    def __init__(self, eng, fn, dma, idx):
        self.eng = eng; self.fn = fn; self.dma = dma; self.idx = idx
        self.deps = set(); self.sig = False; self.sigval = 0; self.sem = None; self.prewait = None


class Sched:
    def __init__(self):
        self.ops = []
        self.lastw = {}
        self.readers = {}

    def add(self, eng, fn, reads=(), writes=(), dma=False):
        idx = len(self.ops)
        op = Op(eng, fn, dma, idx)
        ops = self.ops
        wset = set(writes)
        for k in reads:
            w = self.lastw.get(k)
            if w is not None:
                op.deps.add(w)
        for k in wset:
            w = self.lastw.get(k)
            if w is not None:
                op.deps.add(w)
            for r in self.readers.get(k, {}).values():
                for ri in (r if isinstance(r, list) else [r]):
                    ro = ops[ri]
                    if ro.eng == eng and not ro.dma and not dma:
                        continue
                    op.deps.add(ri)
        for k in wset:
            self.lastw[k] = idx
            self.readers[k] = {}
        for k in reads:
            if k in wset:
                continue
            rd = self.readers.setdefault(k, {})
            if dma:
                rd.setdefault(('dma', eng), []).append(idx)
            else:
                rd[eng] = idx
        if eng == 'pe':
            op.deps = {d for d in op.deps if not (ops[d].eng == 'pe')}
        ops.append(op)
        return idx

    def lower(self, nc, sems, dma_sems):
        ops = self.ops
        for op in ops:
            for d in op.deps:
                ops[d].sig = True
        cnt = {e: 0 for e in sems}
        dcount = {}
        drr = {q: 0 for q in dma_sems}
        for op in ops:
            if op.dma:
                pool = dma_sems[op.eng]
                m = drr[op.eng] % len(pool)
                drr[op.eng] += 1
                sem = pool[m]
                prev = dcount.get((op.eng, m), 0)
                op.prewait = (sem, 16 * prev) if prev > 0 else None
                dcount[(op.eng, m)] = prev + 1
                op.sem = sem
                op.sigval = 16 * (prev + 1)
            elif op.sig:
                cnt[op.eng] += 1
                op.sem = sems[op.eng]
                op.sigval = cnt[op.eng]
        streams = {}
        waited = {}
        for op in ops:
            st = streams.setdefault(op.eng, [])
            need = {}
            if op.prewait is not None:
                need[id(op.prewait[0])] = op.prewait
            for d in op.deps:
                do = ops[d]
                key = id(do.sem)
                if key not in need or need[key][1] < do.sigval:
                    need[key] = (do.sem, do.sigval)
            waits = []
            for key, (sem, val) in need.items():
                wk = (op.eng, key)
                if waited.get(wk, 0) >= val:
                    continue
                waited[wk] = val
                waits.append((sem, val))
            sig = None
            if op.dma:
                sig = (op.sem, 16)
            elif op.sig:
                sig = (op.sem, 1)
            st.append((waits, op, sig))
        finals = {}
        for (q, m), c in dcount.items():
            finals.setdefault(q, []).append((dma_sems[q][m], 16 * c))
        return streams, finals


class Rot:
    def __init__(self, n):
        self.n = n; self.i = -1

    def next(self):
        self.i = (self.i + 1) % self.n
        return self.i


def build_program(n_seq=SEQ_PER_CORE, stop_after=None):
    nc = bass.Bass("TRN2", target_bir_lowering=False)
    ffn_offs, ffn_total = ffn_piece_offsets()
    x_d = nc.dram_tensor("x", [n_seq, SEQ, D], F32, kind="ExternalInput").ap()
    meta_d = nc.dram_tensor("meta", [NMETA, D], F32, kind="ExternalInput").ap()
    cst_d = nc.dram_tensor("cst", [128, C_TOTAL], F32, kind="ExternalInput").ap()
    f1_d = nc.dram_tensor("ffn1", [128, ffn_total], F32, kind="ExternalInput").ap()
    f2_d = nc.dram_tensor("ffn2", [128, ffn_total], F32, kind="ExternalInput").ap()
    mx_d = nc.dram_tensor("mixw", [128, MIX_TOTAL], F32, kind="ExternalInput").ap()
    y_d = nc.dram_tensor("y", [n_seq, SEQ, D], F32, kind="ExternalOutput").ap()
    if stop_after == 'mix':
        dbg_d = nc.dram_tensor("dbg", [128, 1024], F32, kind="ExternalOutput").ap()
        dbg2_d = nc.dram_tensor("dbg2", [128, NT * 128 + L], F32, kind="ExternalOutput").ap()

    S = Sched()
    from contextlib import ExitStack
    es = ExitStack()

    def sb(name, shape, dt):
        return es.enter_context(nc.sbuf_tensor(name, shape, dt))

    with es:
        h = sb("h", [128, NT, D], F32)
        xT = sb("xT", [128, 8, L], BF16)
        cst = sb("cst_sb", [128, C_TOTAL], F32)
        ident = sb("ident", [128, 128], BF16)
        maskb = sb("maskb", [128, 128], BF16)
        zeros = sb("zeros", [128, 264], BF16)
        ring = sb("ring", [128, N_SLOTS, SLOT_ELEMS], BF16)
        ss = sb("ss", [128, NT], F32)
        lnv = sb("lnv", [128, NT], F32)
        rstd = sb("rstd", [128, NT], F32)
        fence = sb("fence", [128, 8], F32)
        xs = sb("xs", [128, 2, D], BF16)
        sg = sb("sg", [128, 2, 512], F32)
        actT = sb("actT", [128, 2, 4, 512], BF16)
        zt = sb("zt", [128, NT, 8], F32)
        Cc = sb("Cc", [128, NT + 1, 8], F32)
        Ft = sb("Ft", [128, NT, 8], F32)
        NPAIR = 9
        Rp = sb("Rp", [128, NPAIR, 8], F32)
        biast = sb("biast", [128, 2, NPAIR, NT], F32)
        qT = sb("qT", [128, L], BF16)
        kA = sb("kA", [128, L], BF16)
        kB = sb("kB", [128, L], BF16)
        vaug = sb("vaug", [128, NT, 2, 65], BF16)
        ub = sb("ub", [128, 514], F32)
        cS = sb("cS", [128, 512], F32)
        acc = sb("acc", [128, 512], F32)
        yc = sb("yc", [128, 512], F32)
        yT = sb("yT", [128, 2, L], BF16)
        pT = sb("pT", [128, 3, 512], BF16)
        junk = pT[:, 0:2, :].rearrange("p a b -> p (a b)")
        lsp = zt
        sqo = cS[:, 0:256].rearrange("p (a b) -> p a b", b=64)
        yv = cS[:, 256:512].rearrange("p (a b) -> p a b", b=64)
        ysq = cS
        lnr = acc
        rr = acc
        ytok = sb("ytok", [128, NT, 128], BF16)
        sso = sb("sso", [128, 4], F32)
        zc = sb("zc", [128, 4], F32)
        z2 = sb("z2", [128, 4], F32)
        lnt = sb("lnt", [128, 4], F32)
        ro = sb("ro", [128, 4], F32)

        psA = [es.enter_context(nc.psum_tensor(f"psA{i}", [128, 512], F32)) for i in range(4)]
        psB = [es.enter_context(nc.psum_tensor(f"psB{i}", [128, 1024], F32)) for i in range(2)]

        def bank(b):
            if b < 4:
                return psA[b][:, :]
            return psB[(b - 4) // 2][:, ((b - 4) % 2) * 512:((b - 4) % 2) * 512 + 512]

        rotA = Rot(4)
        rotB = Rot(2)
        rotO = Rot(2)
        rot_xs = Rot(2); rot_sg = Rot(2); rot_act = Rot(2); rot_pT = Rot(3)
        ring_rot = Rot(N_SLOTS)

        def cc(col, n=1):
            return cst[:, col:col + n]

        def A(fn, reads, writes):
            S.add('act', fn, reads, writes)

        def V(fn, reads, writes):
            S.add('dve', fn, reads, writes)

        def P(fn, reads, writes):
            S.add('pe', fn, reads, writes)

        def act_fence(reads):
            A(lambda e: e.activation(out=fence[:, 0:1], in_=fence[:, 1:2], func=AF.Copy),
              list(reads) + ['fence_in'], ['fence_out'])

        S.add('sp', lambda e: e.dma_start(out=cst[:, :], in_=cst_d[:, :]), [], ['cst'], dma=True)
        V(lambda e: e.tensor_copy(out=ident[:, :], in_=cst[:, C_IDENT:C_IDENT + 128]), ['cst'], ['ident'])
        V(lambda e: e.tensor_copy(out=maskb[:, :], in_=cst[:, C_MASK:C_MASK + 128]), ['cst'], ['maskb'])
        V(lambda e: e.memset(zeros[:, :], 0.0), [], ['zeros'])
        V(lambda e: e.memset(fence[:, :], 0.0), [], ['fence_in', 'fence_out'])
        V(lambda e: e.memset(ss[:, :], 1.0), [], [('ss', g_) for g_ in range(5)])
        V(lambda e: e.memset(zt[:, :, :], 0.0), [], ['zt'])
        V(lambda e: e.memset(Cc[:, :, :], 0.0), [], ['Cc'])
        V(lambda e: e.memset(ub[:, :], 0.0), [], ['ub0'])
        V(lambda e: e.memset(vaug[:, :, :, :], 1.0), [], [('vaug', j_) for j_ in range(NT)])
        V(lambda e: e.memset(h[:, 0, :], 0.0), [], [('h', 0)])
        V(lambda e: e.memset(ytok[:, :, :], 0.0), [], ['ytok'])
        V(lambda e: e.memset(Ft[:, :, :], 0.0), [], ['Ft'])
        V(lambda e: e.memset(lnv[:, :], 0.0), [], [('lnv', g_) for g_ in range(5)])
        V(lambda e: e.memset(rstd[:, :], 1.0), [], [('rstd', g_) for g_ in range(5)])
        V(lambda e: e.memset(kA[:, :], 0.0), [], [('kT', g_) for g_ in range(5)])
        V(lambda e: e.memset(kB[:, :], 0.0), [], [('kT', g_) for g_ in range(5)])

        def load_piece(src, off, n):
            slot = ring_rot.next()
            assert n <= SLOT_ELEMS
            S.add('pool', lambda e: e.dma_start(out=ring[:, slot, 0:n], in_=src[:, off:off + n],
                                                max_dma_last_dim=8192),
                  [], [('ring', slot)], dma=True)
            return slot

        def norm_front(g, gcol):
            tl = TOK_GROUPS[g]
            j0, j1 = tl[0], tl[-1] + 1
            for j in tl:
                R = tile_rows(j)
                A(lambda e, j=j, R=R: e.activation(out=junk[:R, :], in_=h[:R, j, :], func=AF.Square,
                                                   accum_out=ss[:R, j:j + 1]),
                  [('h', j)], [('pT', 0), ('pT', 1), ('ss', g)])
            act_fence([('ss', g)])
            A(lambda e: e.activation(out=lnv[:, j0:j1], in_=ss[:, j0:j1], func=AF.Ln, scale=1.0 / D, bias=epst[:, 0:1]),
              [('ss', g), 'fence_out', 'epst'], [('lnv', g)])
            A(lambda e: e.activation(out=rstd[:, j0:j1], in_=lnv[:, j0:j1], func=AF.Exp, scale=-0.5), [('lnv', g)], [('rstd', g)])

            def back():
                for j in tl:
                    R = tile_rows(j)
                    c0 = tile_col0(j)
                    xb = rot_xs.next()
                    if j % 2 == 0:
                        A(lambda e, j=j, R=R, xb=xb: e.activation(out=xs[:R, xb, :], in_=h[:R, j, :], func=AF.Copy,
                                                                  scale=rstd[:R, j:j + 1]),
                          [('h', j), ('rstd', g)], [('xs', xb)])
                    else:
                        V(lambda e, j=j, R=R, xb=xb: e.tensor_scalar(out=xs[:R, xb, :], in0=h[:R, j, :], scalar1=rstd[:R, j:j + 1],
                                                                     scalar2=None, op0=ALU.mult),
                          [('h', j), ('rstd', g)], [('xs', xb)])
                    b = rotA.next()
                    tp = bank(b).bitcast(BF16)
                    tp3 = tp.rearrange('p (k t) -> p k t', t=128)
                    for k in range(8):
                        P(lambda e, k=k, R=R, xb=xb, tp3=tp3: e.transpose(out=tp3[:, k, :R],
                                                                        in_=xs[:R, xb, k * 128:(k + 1) * 128],
                                                                        identity=ident[:R, :R]),
                          [('xs', xb), 'ident'], [('ps', b)])
                    gbc = cst[:, gcol:gcol + 8].unsqueeze(2).to_broadcast([128, 8, R])
                    V(lambda e, R=R, c0=c0, tp3=tp3, gbc=gbc: e.tensor_tensor(out=xT[:, :, c0:c0 + R], in0=tp3[:, :, :R],
                                                                              in1=gbc, op=ALU.mult),
                      [('ps', b), 'cst'], [('xT', j)])
            return back

        def norm_to_xT(gcol):
            for g in range(len(TOK_GROUPS)):
                norm_front(g, gcol)()

        epst = sb("epst", [128, 1], F32)
        V(lambda e: e.memset(epst[:, :], EPS), [], ['epst'])
        onet = sb("onet", [128, 1], F32)
        V(lambda e: e.memset(onet[:, :], 1.0), [], ['onet'])

        def ffn(src_d, tail=None, head=None):
            for gi, (c0, n) in enumerate(FF_GROUPS):
                last_grp = (gi == len(FF_GROUPS) - 1)
                deferred = None
                (go, gl), (uo, ul), (do_, dl) = ffn_offs[gi]
                sg_slot = load_piece(src_d, go, gl)
                su_slot = load_piece(src_d, uo, ul)
                sd_slot = load_piece(src_d, do_, dl)
                wg = ring[:, sg_slot, 0:gl].rearrange('p (k c) -> p k c', k=8)
                wu = ring[:, su_slot, 0:ul].rearrange('p (k c) -> p k c', k=8)
                wd = ring[:, sd_slot, 0:dl].rearrange('p (i c) -> p i c', c=D)
                for g in range(len(TOK_GROUPS)):
                    t0, t1_ = group_cols(g)
                    T = t1_ - t0
                    xkeys = [('xT', j) for j in TOK_GROUPS[g]]
                    ab = rot_act.next()
                    for i in range(n):
                        bg = rotA.next(); bu = rotA.next()
                        for k in range(8):
                            P(lambda e, i=i, k=k, bg=bg, wg=wg, t0=t0, T=T: e.matmul(
                                out=bank(bg)[:, :T], lhsT=wg[:, k, i * 128:(i + 1) * 128], rhs=xT[:, k, t0:t0 + T],
                                start=(k == 0), stop=(k == 7)),
                              xkeys + [('ring', sg_slot)], [('ps', bg)])
                        for k in range(8):
                            P(lambda e, i=i, k=k, bu=bu, wu=wu, t0=t0, T=T: e.matmul(
                                out=bank(bu)[:, :T], lhsT=wu[:, k, i * 128:(i + 1) * 128], rhs=xT[:, k, t0:t0 + T],
                                start=(k == 0), stop=(k == 7)),
                              xkeys + [('ring', su_slot)], [('ps', bu)])
                        sb_ = rot_sg.next()
                        A(lambda e, bg=bg, sb_=sb_, T=T: e.activation(out=sg[:, sb_, :T], in_=bank(bg)[:, :T], func=AF.Silu),
                          [('ps', bg)], [('sg', sb_)])
                        V(lambda e, bu=bu, sb_=sb_, ab=ab, i=i, T=T: e.tensor_tensor(
                            out=actT[:, ab, i, :T], in0=bank(bu)[:, :T], in1=sg[:, sb_, :T], op=ALU.mult),
                          [('ps', bu), ('sg', sb_)], [('actT', ab, i)])
                    if deferred is not None:
                        deferred(); deferred = None
                    if gi == 0 and head is not None and g + 1 < len(TOK_GROUPS):
                        head(g + 1)
                    for j in TOK_GROUPS[g]:
                        R = tile_rows(j)
                        lc = tile_col0(j) - t0
                        pb = rotB.next()
                        dps = psB[pb]
                        for half in range(2):
                            for i in range(n):
                                P(lambda e, i=i, half=half, dps=dps, ab=ab, lc=lc, R=R, wd=wd, n=n: e.matmul(
                                    out=dps[:R, half * 512:(half + 1) * 512], lhsT=actT[:, ab, i, lc:lc + R],
                                    rhs=wd[:, i, half * 512:(half + 1) * 512], start=(i == 0), stop=(i == n - 1)),
                                  [('actT', ab, i), ('ring', sd_slot)], [('ps', 4 + 2 * pb + half)])
                        V(lambda e, dps=dps, R=R, j=j: e.scalar_tensor_tensor(
                            out=h[:R, j, :], in0=dps[:R, :], scalar=0.5, in1=h[:R, j, :], op0=ALU.mult, op1=ALU.add),
                          [('ps', 4 + 2 * pb), ('ps', 5 + 2 * pb), ('h', j)], [('h', j)])
                    if last_grp and tail is not None:
                        deferred = tail(g)
                if deferred is not None:
                    deferred(); deferred = None

        def wout_partial(wo_aps, wo_keys, tail=None):
            deferred = None
            for j in range(NT):
                R = tile_rows(j)
                c0 = tile_col0(j)
                pb = rotB.next()
                dps = psB[pb]
                for half in range(2):
                    for s_ in range(2):
                        P(lambda e, s_=s_, half=half, dps=dps, c0=c0, R=R: e.matmul(
                            out=dps[:R, half * 512:(half + 1) * 512], lhsT=yT[:, s_, c0:c0 + R],
                            rhs=wo_aps[s_][:, half * 512:(half + 1) * 512], start=(s_ == 0), stop=(s_ == 1)),
                          [('yT', s_), wo_keys[s_]], [('ps', 4 + 2 * pb + half)])
                V(lambda e, dps=dps, R=R, j=j: e.tensor_tensor(out=h[:R, j, :], in0=dps[:R, :], in1=h[:R, j, :], op=ALU.add),
                  [('ps', 4 + 2 * pb), ('ps', 5 + 2 * pb), ('h', j)], [('h', j)])
                gdone = [g_ for g_, tl_ in enumerate(TOK_GROUPS) if tl_[-1] == j]
                if gdone:
                    if deferred is not None:
                        deferred(); deferred = None
                    if tail is not None:
                        deferred = tail(gdone[0])
            if deferred is not None:
                deferred()

        def wout_items(wo_aps, wo_keys):
            items = []
            for j in range(NT):
                def it(j=j):
                    R = tile_rows(j); c0 = tile_col0(j)
                    for half in range(2):
                        b = rotA.next()
                        for s_ in range(2):
                            P(lambda e, s_=s_, half=half, b=b: e.matmul(
                                out=bank(b)[:R, :], lhsT=yT[:, s_, c0:c0 + R],
                                rhs=wo_aps[s_][:, half * 512:(half + 1) * 512], start=(s_ == 0), stop=(s_ == 1)),
                              [('yT', s_), wo_keys[s_]], [('ps', b)])
                        V(lambda e, half=half, b=b: e.tensor_tensor(out=h[:R, j, half * 512:(half + 1) * 512], in0=bank(b)[:R, :],
                                                                    in1=h[:R, j, half * 512:(half + 1) * 512], op=ALU.add),
                          [('ps', b), ('h', j)], [('h', j)])
                items.append(it)
            return items

        def mixer(tail=None):
            sfg = load_piece(mx_d, 0, MIX_FG)
            wfg = ring[:, sfg, 0:MIX_FG].rearrange('p (k c) -> p k c', k=8)
            for j in range(NT):
                R = tile_rows(j); c0 = tile_col0(j)
                b = rotA.next()
                for k in range(8):
                    P(lambda e, k=k, b=b, R=R, c0=c0: e.matmul(out=bank(b)[:R, 0:8], lhsT=xT[:, k, c0:c0 + R],
                                                              rhs=wfg[:, k, :], start=(k == 0), stop=(k == 7)),
                      [('xT', j), ('ring', sfg)], [('ps', b)])
                V(lambda e, b=b, R=R, j=j: e.tensor_tensor(out=zt[:R, j, :], in0=bank(b)[:R, 0:8],
                                                           in1=cst[:R, C_BF:C_BF + 8], op=ALU.add),
                  [('ps', b), 'cst'], ['zt'])
            A(lambda e: e.activation(out=zt[:, :, :], in_=zt[:, :, :], func=AF.Exp, scale=-1.0), ['zt'], ['zt'])
            A(lambda e: e.activation(out=zt[:, :, :], in_=zt[:, :, :], func=AF.Ln, bias=onet[:, 0:1]), ['zt', 'onet'], ['zt', 'lsp'])
            for j in range(NT):
                R = tile_rows(j)
                b1 = rotA.next(); b2 = rotA.next()
                P(lambda e, b1=b1, R=R, j=j: e.matmul(out=bank(b1)[:R, 0:8], lhsT=cst[:R, C_NEGTRI:C_NEGTRI + R],
                                                      rhs=lsp[:R, j, :], start=True, stop=True),
                  ['lsp', 'cst'], [('ps', b1)])
                P(lambda e, b2=b2, R=R, j=j: e.matmul(out=bank(b2)[:, 0:8], lhsT=cst[:R, C_NEGONES:C_NEGONES + 128],
                                                      rhs=lsp[:R, j, :], start=True, stop=True),
                  ['lsp', 'cst'], [('ps', b2)])
                V(lambda e, b1=b1, R=R, j=j: e.tensor_tensor(out=Ft[:R, j, :], in0=bank(b1)[:R, 0:8], in1=Cc[:R, j, :], op=ALU.add),
                  [('ps', b1), 'Cc'], ['Ft'])
                V(lambda e, b2=b2, j=j: e.tensor_tensor(out=Cc[:, j + 1, :], in0=bank(b2)[:, 0:8], in1=Cc[:, j, :], op=ALU.add),
                  [('ps', b2), 'Cc'], ['Cc'])
            V(lambda e: e.tensor_copy(out=Rp[:, 0, :], in_=Cc[:, 1, :]), ['Cc'], ['Rp'])
            V(lambda e: e.tensor_copy(out=Rp[:, 1:NPAIR, :], in_=Cc[:, 2:NT + 1, :].rearrange('p (a two) h -> p a two h', two=2)[:, :, 0, :]),
              ['Cc'], ['Rp'])
            pending_fin = []

            ALIAS_KEYS = [('actT', a_, i_) for a_ in range(2) for i_ in range(4)] + [('sg', 0), ('sg', 1)]
            SET_KEYS = [['ub0', 'cS', 'acc', 'yc'], ['ub1', 'cS1', 'acc1', 'yc1']]
            V(lambda e: e.memset(fence[:, 2:3], 0.0), [], ALIAS_KEYS + SET_KEYS[1])
            actF = actT[:, :, :, :].rearrange('p a b c -> p (a b c)').bitcast(F32)
            sgF = sg[:, :, :].rearrange('p a b -> p (a b)')
            ubs = [ub, actF[:, 0:514]]
            cSs = [cS, actF[:, 514:1026]]
            accs = [acc, actF[:, 1026:1538]]
            ycs = [yc, sgF[:, 0:512]]
            it = 0
            pend_back = None
            for pair in range(2):
                wo_aps = []; wo_keys = []
                for s_ in range(2):
                    cch = pair * 2 + s_
                    off = MIX_FG + cch * MIX_UNIT
                    sw = load_piece(mx_d, off, MIX_UNIT)
                    wcv = ring[:, sw, 0:3072].rearrange('p (k t c) -> p k t c', k=8, t=3)
                    wo_aps.append(ring[:, sw, 3072:4096]); wo_keys.append(('ring', sw))
                    for g in range(len(TOK_GROUPS)):
                        t0, t1_ = group_cols(g); T = t1_ - t0
                        xkeys = [('xT', j) for j in TOK_GROUPS[g]]
                        p = it % 2; it += 1
                        kub, kcS, kacc, kyc = SET_KEYS[p]
                        ub_, cS_, acc_, yc_ = ubs[p], cSs[p], accs[p], ycs[p]
                        ubprev, kubprev = ubs[1 - p], SET_KEYS[1 - p][0]
                        bc_ = rotA.next(); bh_ = rotA.next(); bb_ = 4 + p; bst = 6 + p
                        for which, bk in ((1, bc_), (2, bh_), (0, bb_)):
                            for k in range(8):
                                P(lambda e, k=k, bk=bk, which=which, wcv=wcv, t0=t0, T=T: e.matmul(
                                    out=bank(bk)[:, :T], lhsT=wcv[:, k, which, :], rhs=xT[:, k, t0:t0 + T],
                                    start=(k == 0), stop=(k == 7)),
                                  xkeys + [('ring', sw)], [('ps', bk)])
                        A(lambda e, bc_=bc_, T=T, cS_=cS_: e.activation(out=cS_[:, :T], in_=bank(bc_)[:, :T], func=AF.Copy),
                          [('ps', bc_)], [kcS])
                        if g == 0:
                            V(lambda e, ub_=ub_: e.memset(ub_[:, 0:2], 0.0), [], [kub])
                        else:
                            V(lambda e, Tp=Tprev, ub_=ub_, ubprev=ubprev: e.tensor_copy(out=ub_[:, 0:2], in_=ubprev[:, Tp:Tp + 2]),
                              [kubprev], [kub])
                        Tprev = T
                        V(lambda e, bh_=bh_, T=T, ub_=ub_, cS_=cS_: e.tensor_tensor(out=ub_[:, 2:2 + T], in0=bank(bh_)[:, :T],
                                                                                  in1=cS_[:, :T], op=ALU.mult),
                          [('ps', bh_), kcS], [kub])
                        w0 = cst[:, C_CONVW + cch * 3 + 0:C_CONVW + cch * 3 + 1]
                        w1 = cst[:, C_CONVW + cch * 3 + 1:C_CONVW + cch * 3 + 2]
                        w2 = cst[:, C_CONVW + cch * 3 + 2:C_CONVW + cch * 3 + 3]
                        V(lambda e, T=T, w2=w2, ub_=ub_, acc_=acc_: e.tensor_scalar(out=acc_[:, :T], in0=ub_[:, 2:2 + T], scalar1=w2,
                                                                                 scalar2=None, op0=ALU.mult), [kub, 'cst'], [kacc])
                        V(lambda e, T=T, w1=w1, ub_=ub_, acc_=acc_: e.scalar_tensor_tensor(out=acc_[:, :T], in0=ub_[:, 1:1 + T], scalar=w1,
                                                                                        in1=acc_[:, :T], op0=ALU.mult, op1=ALU.add),
                          [kub, 'cst', kacc], [kacc])
                        V(lambda e, T=T, w0=w0, ub_=ub_, acc_=acc_: e.scalar_tensor_tensor(out=acc_[:, :T], in0=ub_[:, 0:T], scalar=w0,
                                                                                        in1=acc_[:, :T], op0=ALU.mult, op1=ALU.add),
                          [kub, 'cst', kacc], [kacc])
                        V(lambda e, bb_=bb_, T=T, yc_=yc_, acc_=acc_: e.tensor_tensor(out=yc_[:, :T], in0=bank(bb_)[:, :T], in1=acc_[:, :T], op=ALU.mult),
                          [('ps', bb_), kacc], [kyc])
                        A(lambda e, T=T, yc_=yc_, cS_=cS_: e.activation(out=cS_[:, :T], in_=yc_[:, :T], func=AF.Square), [kyc], [kcS])

                        def back(T=T, t0=t0, s_=s_, cch=cch, bst=bst, cS_=cS_, acc_=acc_, yc_=yc_, kcS=kcS, kacc=kacc, kyc=kyc):
                            P(lambda e: e.matmul(out=bank(bst)[:, :T], lhsT=cst[:, C_BLK:C_BLK + 128], rhs=cS_[:, :T],
                                                 start=True, stop=True), [kcS, 'cst'], [('ps', bst)])
                            A(lambda e: e.activation(out=acc_[:, :T], in_=bank(bst)[:, :T], func=AF.Ln, bias=epst[:, 0:1]),
                              [('ps', bst), 'epst'], [kacc])
                            A(lambda e: e.activation(out=acc_[:, :T], in_=acc_[:, :T], func=AF.Exp, scale=-0.5), [kacc], [kacc])
                            gcv = cst[:, C_GCONV + cch:C_GCONV + cch + 1]
                            V(lambda e: e.scalar_tensor_tensor(
                                out=yT[:, s_, t0:t0 + T], in0=yc_[:, :T], scalar=gcv, in1=acc_[:, :T], op0=ALU.mult, op1=ALU.mult),
                              [kyc, kacc, 'cst'], [('yT', s_)])
                        if pend_back is not None:
                            pend_back()
                        pend_back = back
                if pend_back is not None:
                    pend_back(); pend_back = None
                if pair == 0:
                    wout_partial(wo_aps, wo_keys)
                else:
                    bg_work = wout_items(wo_aps, wo_keys)
            V(lambda e: e.memset(fence[:, 3:4], 0.0), [], ALIAS_KEYS + SET_KEYS[1])

            for pair in range(2):
                wo_aps = []; wo_keys = []
                for s_ in range(2):
                    hp = pair * 2 + s_
                    off = MIX_FG + (4 + hp) * MIX_UNIT
                    sw = load_piece(mx_d, off, MIX_UNIT)
                    wqkv = ring[:, sw, 0:3072].rearrange('p (k t c) -> p k t c', k=8, t=3)
                    wo_aps.append(ring[:, sw, 3072:4096]); wo_keys.append(('ring', sw))
                    def proj_items(g, wqkv=wqkv, sw=sw):
                        items = []
                        t0, t1_ = group_cols(g); T = t1_ - t0
                        xkeys = [('xT', j) for j in TOK_GROUPS[g]]
                        for which, bkk in ((0, 6), (1, 7)):
                            for k in range(8):
                                def it(k=k, bkk=bkk, which=which):
                                    P(lambda e: e.matmul(
                                        out=bank(bkk)[:, :T], lhsT=wqkv[:, k, which, :], rhs=xT[:, k, t0:t0 + T],
                                        start=(k == 0), stop=(k == 7)),
                                      xkeys + [('ring', sw)], [('ps', bkk)])
                                    if k == 7 and which == 0:
                                        V(lambda e: e.tensor_scalar(out=qT[:, t0:t0 + T], in0=bank(6)[:, :T], scalar1=0.125,
                                                                    scalar2=None, op0=ALU.mult),
                                          [('ps', 6)], [('qT', g)])
                                    if k == 7 and which == 1:
                                        V(lambda e: e.tensor_copy(out=kA[0:64, t0:t0 + T], in_=bank(7)[0:64, :T]), [('ps', 7)], [('kT', g)])
                                        V(lambda e: e.tensor_copy(out=kB[64:128, t0:t0 + T], in_=bank(7)[64:128, :T]), [('ps', 7)], [('kT', g)])
                                items.append(it)
                        for n_, j in enumerate(TOK_GROUPS[g]):
                            R = tile_rows(j); c0 = tile_col0(j)
                            bv = 6 + (n_ % 2)
                            for k in range(8):
                                def it(k=k, bv=bv, R=R, c0=c0, j=j):
                                    P(lambda e: e.matmul(out=bank(bv)[:R, 0:128], lhsT=xT[:, k, c0:c0 + R],
                                                         rhs=wqkv[:, k, 2, :], start=(k == 0), stop=(k == 7)),
                                      [('xT', j), ('ring', sw)], [('ps', bv)])
                                    if k == 7:
                                        V(lambda e: e.tensor_copy(out=vaug[:R, j, :, 0:64],
                                                                  in_=bank(bv)[:R, 0:128].rearrange('p (a d) -> p a d', a=2)),
                                          [('ps', bv)], [('vaug', j)])
                                items.append(it)
                        return items

                    for hh in range(2):
                        hd = hp * 2 + hh
                        rm = Rp[:, :, hd].unsqueeze(2).to_broadcast([128, NPAIR, NT])
                        fk = Ft[:, :, hd].unsqueeze(1).to_broadcast([128, NPAIR, NT])
                        V(lambda e, hh=hh, rm=rm, fk=fk: e.tensor_tensor(out=biast[:, hh, :, :], in0=rm, in1=fk, op=ALU.subtract),
                          ['Rp', 'Ft'], [('biast', hh)])
                    for it_ in proj_items(0):
                        it_()
                    pend_pv = None
                    for g in range(len(TOK_GROUPS)):
                        work = proj_items(g + 1) if g + 1 < len(TOK_GROUPS) else []
                        n_iter = 2 * (TOK_GROUPS[g][-1] + 1)
                        per_it = -(-len(work) // max(1, n_iter - 2))
                        for hh in range(2):
                            blist = TOK_GROUPS[g]
                            nb = len(blist)
                            ob = 4 + rotO.next()
                            op3 = bank(ob)[:, 0:nb * 65].rearrange('p (b d) -> p b d', d=65)
                            P(lambda e, ob=ob, nb=nb: e.matmul(out=bank(ob)[:, 0:nb * 65], lhsT=zeros[0:1, 0:128], rhs=zeros[0:1, 0:nb * 65],
                                                               start=True, stop=True),
                              ['zeros'], [('ps', ob)])

                            def emit_st(j, blist=blist, g=g, hh=hh):
                                Rk = tile_rows(j); kc0 = tile_col0(j)
                                gj = [g_ for g_, tl_ in enumerate(TOK_GROUPS) if j in tl_][0]
                                bl = [b for b in blist if b >= j]
                                q0 = tile_col0(bl[0]); q1 = tile_col0(bl[-1]) + tile_rows(bl[-1])
                                Tq = q1 - q0
                                bs_ = rotA.next()
                                diag = (j in blist)
                                P(lambda e: e.matmul(
                                    out=bank(bs_)[:Rk, :Tq], lhsT=(kA if hh == 0 else kB)[:, kc0:kc0 + Rk], rhs=qT[:, q0:q0 + Tq],
                                    start=True, stop=(not diag)),
                                  [('kT', gj), ('qT', g)], [('ps', bs_)])
                                if diag:
                                    P(lambda e: e.matmul(out=bank(bs_)[:Rk, :Rk], lhsT=ident[:Rk, :Rk], rhs=maskb[:Rk, :Rk],
                                                         start=False, stop=True),
                                      ['ident', 'maskb'], [('ps', bs_)])
                                pb_ = rot_pT.next()
                                pieces = {}
                                for b in bl:
                                    pi = 0 if g == 0 else 2 * (g - 1) + 1 + blist.index(b) // 2
                                    pieces.setdefault(pi, []).append(b)
                                for pi, bs in pieces.items():
                                    lc0 = tile_col0(bs[0]) - q0
                                    lc1 = tile_col0(bs[-1]) + tile_rows(bs[-1]) - q0
                                    A(lambda e, lc0=lc0, lc1=lc1, pi=pi: e.activation(
                                        out=pT[:Rk, pb_, lc0:lc1], in_=bank(bs_)[:Rk, lc0:lc1], func=AF.Exp,
                                        bias=biast[:Rk, hh, pi, j:j + 1]),
                                      [('ps', bs_), ('biast', hh)], [('pT', pb_)])
                                return (j, Rk, bl, q0, pb_)

                            def emit_pv(st, blist=blist, op3=op3, ob=ob, hh=hh):
                                j, Rk, bl, q0, pb_ = st
                                for b in bl:
                                    Rb = tile_rows(b); lc = tile_col0(b) - q0; bi = blist.index(b)
                                    P(lambda e, Rb=Rb, lc=lc, bi=bi, b=b: e.matmul(
                                        out=op3[:Rb, bi, :], lhsT=pT[:Rk, pb_, lc:lc + Rb], rhs=vaug[:Rk, j, hh, :],
                                        start=False, stop=(j == b), skip_group_check=True),
                                      [('pT', pb_), ('vaug', j)], [('ps', ob)])

                            for j in range(blist[-1] + 1):
                                cur = emit_st(j)
                                if pend_pv is not None:
                                    pend_pv()
                                pend_pv = (lambda cur=cur, emit_pv=emit_pv: emit_pv(cur))
                                if pending_fin and (j == 1 or j == blist[-1]):
                                    pending_fin.pop()()
                                for _ in range(per_it):
                                    if work:
                                        work.pop(0)()
                                if bg_work and j >= 1:
                                    bg_work.pop(0)()

                            def fin(blist=blist, nb=nb, op3=op3, ob=ob, hh=hh):
                                Rg = tile_rows(blist[0])
                                V(lambda e: e.tensor_copy(out=zc[:Rg, :nb], in_=op3[:Rg, :, 64]),
                                  [('ps', ob)], ['zc'])
                                V(lambda e: e.reciprocal(out=z2[:Rg, :nb], in_=zc[:Rg, :nb]), ['zc'], ['z2'])
                                for bi, b in enumerate(blist):
                                    V(lambda e, bi=bi: e.tensor_scalar(
                                        out=yv[:Rg, bi, :], in0=op3[:Rg, bi, 0:64], scalar1=z2[:Rg, bi:bi + 1],
                                        scalar2=None, op0=ALU.mult),
                                      [('ps', ob), 'z2'], ['cS'])
                                V(lambda e: e.tensor_tensor(out=sqo[:Rg, :nb, :], in0=yv[:Rg, :nb, :], in1=yv[:Rg, :nb, :], op=ALU.mult),
                                  ['cS'], ['cS'])
                                V(lambda e: e.reduce_sum(out=sso[:Rg, :nb], in_=sqo[:Rg, :nb, :], axis=AX.X), ['cS'], ['sso'])
                                A(lambda e: e.activation(out=lnt[:Rg, :nb], in_=sso[:Rg, :nb], func=AF.Ln, scale=1.0 / 64,
                                                         bias=epst[:Rg, 0:1]), ['sso', 'epst'], ['lnt'])
                                A(lambda e: e.activation(out=ro[:Rg, :nb], in_=lnt[:Rg, :nb], func=AF.Exp, scale=-0.5), ['lnt'], ['ro'])
                                for bi, b in enumerate(blist):
                                    V(lambda e, bi=bi, b=b: e.tensor_scalar(
                                        out=ytok[:Rg, b, hh * 64:(hh + 1) * 64], in0=yv[:Rg, bi, :], scalar1=ro[:Rg, bi:bi + 1],
                                        scalar2=None, op0=ALU.mult),
                                      ['cS', 'ro'], [('ytok', b)])
                            pending_fin.append(fin)
                        while work:
                            work.pop(0)()
                    if pend_pv is not None:
                        pend_pv(); pend_pv = None
                    if pending_fin:
                        pending_fin.pop()()
                    while bg_work:
                        bg_work.pop(0)()
                    gat = cst[:, C_GATTN + hp:C_GATTN + hp + 1]
                    for j in range(NT):
                        R = tile_rows(j); c0 = tile_col0(j)
                        b = rotA.next()
                        tpv = bank(b).bitcast(BF16)
                        P(lambda e, tpv=tpv, R=R, j=j: e.transpose(out=tpv[:, 0:R], in_=ytok[:R, j, :], identity=ident[:R, :R]),
                          [('ytok', j), 'ident'], [('ps', b)])
                        V(lambda e, tpv=tpv, R=R, c0=c0, s_=s_, gat=gat: e.tensor_scalar(out=yT[:, s_, c0:c0 + R], in0=tpv[:, 0:R], scalar1=gat,
                                                                                       scalar2=None, op0=ALU.mult),
                          [('ps', b), 'cst'], [('yT', s_)])
                if pair == 1:
                    wout_partial(wo_aps, wo_keys, tail=tail)
                else:
                    bg_work = wout_items(wo_aps, wo_keys)

        def final_store(s, do_norm=True):
            if do_norm:
                for j in range(1, NT):
                    A(lambda e, j=j: e.activation(out=junk[:, :], in_=h[:, j, :], func=AF.Square, accum_out=ss[:, j:j + 1]),
                      [('h', j)], [('pT', 0), ('pT', 1), 'ss'])
                act_fence(['ss'])
                A(lambda e: e.activation(out=lnv[:, :], in_=ss[:, :], func=AF.Ln, scale=1.0 / D, bias=epst[:, 0:1]),
                  ['ss', 'fence_out', 'epst'], ['lnv'])
                A(lambda e: e.activation(out=rstd[:, :], in_=lnv[:, :], func=AF.Exp, scale=-0.5), ['lnv'], ['rstd'])
            for j in range(1, NT):
                if do_norm:
                    V(lambda e, j=j: e.scalar_tensor_tensor(out=h[:, j, :], in0=h[:, j, :], scalar=rstd[:, j:j + 1],
                                                            in1=cst[:, C_GFIN:C_GFIN + D], op0=ALU.mult, op1=ALU.mult),
                      [('h', j), 'rstd', 'cst'], [('h', j)])
                S.add('sp', lambda e, j=j: e.dma_start(out=y_d[s, (j - 1) * 128:j * 128, :], in_=h[:, j, :]),
                      [('h', j)], [], dma=True)

        def load_group(s, g):
            for j in TOK_GROUPS[g]:
                if j == 0:
                    S.add('sp', lambda e: e.dma_start(out=h[0:NMETA, 0, :], in_=meta_d[:, :]), [], [('h', 0)], dma=True)
                else:
                    S.add('sp', lambda e, j=j, s=s: e.dma_start(out=h[:, j, :], in_=x_d[s, (j - 1) * 128:j * 128, :]),
                          [], [('h', j)], dma=True)

        def final_group(s, g):
            tl = [j for j in TOK_GROUPS[g] if j >= 1]
            if not tl:
                return
            j0, j1 = tl[0], tl[-1] + 1
            for j in tl:
                A(lambda e, j=j: e.activation(out=junk[:, :], in_=h[:, j, :], func=AF.Square, accum_out=ss[:, j:j + 1]),
                  [('h', j)], [('pT', 0), ('pT', 1), ('ss', g)])
            act_fence([('ss', g)])
            A(lambda e: e.activation(out=lnv[:, j0:j1], in_=ss[:, j0:j1], func=AF.Ln, scale=1.0 / D, bias=epst[:, 0:1]),
              [('ss', g), 'fence_out', 'epst'], [('lnv', g)])
            A(lambda e: e.activation(out=rstd[:, j0:j1], in_=lnv[:, j0:j1], func=AF.Exp, scale=-0.5), [('lnv', g)], [('rstd', g)])
            for j in tl:
                V(lambda e, j=j: e.scalar_tensor_tensor(out=h[:, j, :], in0=h[:, j, :], scalar=rstd[:, j:j + 1],
                                                        in1=cst[:, C_GFIN:C_GFIN + D], op0=ALU.mult, op1=ALU.mult),
                  [('h', j), ('rstd', g), 'cst'], [('h', j)])
                S.add('sp', lambda e, j=j: e.dma_start(out=y_d[s, (j - 1) * 128:j * 128, :], in_=h[:, j, :]),
                      [('h', j)], [], dma=True)

        dbg_mode = stop_after is not None
        for s in range(n_seq):
            if s == 0 or dbg_mode:
                for g in range(len(TOK_GROUPS)):
                    load_group(s, g)
            if dbg_mode:
                norm_to_xT(C_G1)
            if dbg_mode:
                ffn(f1_d)
                if stop_after == 'ffn1':
                    final_store(s, do_norm=False); continue
                norm_to_xT(C_GM)
                mixer()
                if stop_after == 'mix':
                    final_store(s, do_norm=False); continue
                norm_to_xT(C_G2)
                ffn(f2_d)
                final_store(s, do_norm=False); continue
            import os
            if os.environ.get('NO_TAILS'):
                if s > 0:
                    for g in range(len(TOK_GROUPS)):
                        load_group(s, g)
                norm_to_xT(C_G1)
                ffn(f1_d)
                norm_to_xT(C_GM)
                mixer()
                norm_to_xT(C_G2)
                ffn(f2_d)
                for g in range(len(TOK_GROUPS)):
                    final_group(s, g)
                continue
            norm_front(0, C_G1)()
            ffn(f1_d, tail=lambda g: norm_front(g, C_GM), head=lambda g: norm_front(g, C_G1)())
            mixer(tail=lambda g: norm_front(g, C_G2))
            if not os.environ.get('TAIL2_ON'):
                ffn(f2_d)
                for g in range(len(TOK_GROUPS)):
                    final_group(s, g)
                if s + 1 < n_seq:
                    for g in range(len(TOK_GROUPS)):
                        load_group(s + 1, g)
                continue

            st2 = {'pg': None}

            def tail2(g, s=s, st2=st2):
                final_group(s, g)
                if s + 1 < n_seq:
                    load_group(s + 1, g)
                    ret = None
                    if st2['pg'] is not None:
                        ret = norm_front(st2['pg'], C_G1)
                    st2['pg'] = g
                    return ret
                return None
            ffn(f2_d, tail=tail2)
            if st2['pg'] is not None:
                norm_front(st2['pg'], C_G1)()

        sem_ctx = {e: es.enter_context(nc.semaphore(f"tl_{e}")) for e in COMPUTE}
        dma_sems = {
            'sp': [es.enter_context(nc.semaphore(f"dsp{i}")) for i in range(24)],
            'pool': [es.enter_context(nc.semaphore(f"dpl{i}")) for i in range(8)],
        }
        streams, finals = S.lower(nc, sem_ctx, dma_sems)
        block = es.enter_context(nc.Block())

        def emit(engname, tail_waits=()):
            def f(e):
                for waits, op, sig in streams.get(engname, []):
                    for (sem, val) in waits:
                        e.wait_ge(sem, val)
                    ins = op.fn(e)
                    if sig is not None:
                        ins.then_inc(sig[0], sig[1])
                for (sem, val) in tail_waits:
                    e.wait_ge(sem, val)
            return f

        block.sync(emit('sp', finals.get('sp', [])))
        block.gpsimd(emit('pool', finals.get('pool', [])))
        block.tensor(emit('pe'))
        block.scalar(emit('act'))
        block.vector(emit('dve'))
    return nc


_PROG = {}


def kernel(**inputs):
    inp = {k: np.asarray(v) for k, v in inputs.items()}
    x = np.ascontiguousarray(inp['x'], dtype=np.float32)
    cst = make_consts(inp)
    f1 = ffn_layout(inp['ffn1_w_gu'].astype(np.float32), inp['ffn1_w_down'].astype(np.float32))
    f2 = ffn_layout(inp['ffn2_w_gu'].astype(np.float32), inp['ffn2_w_down'].astype(np.float32))
    mx = mix_layout(inp['w_in'].astype(np.float32), inp['w_out'].astype(np.float32))
    meta = np.ascontiguousarray(inp['meta_tokens'], dtype=np.float32)
    if 'nc' not in _PROG:
        _PROG['nc'] = build_program(SEQ_PER_CORE)
    nc = _PROG['nc']
    in_maps = []
    for c in range(N_CORES):
        in_maps.append({"x": x[c * SEQ_PER_CORE:(c + 1) * SEQ_PER_CORE], "meta": meta, "cst": cst,
                        "ffn1": f1, "ffn2": f2, "mixw": mx})
    res = run_bass_kernel_spmd(nc, in_maps, core_ids=list(range(N_CORES)))
    out = np.concatenate([np.asarray(r["y"]) for r in res.results], axis=0)
    return out.astype(np.float32)
```

```python
import numpy as np
import concourse.bass as bass
import concourse.mybir as mybir
from concourse.bass_utils import run_bass_kernel_spmd

F32 = mybir.dt.float32
BF16 = mybir.dt.bfloat16
AF = mybir.ActivationFunctionType
ALU = mybir.AluOpType
AX = mybir.AxisListType

D = 1024
SEQ = 2048
NMETA = 16
L = SEQ + NMETA
NT = 17
DFF = 2816
NCH = DFF // 128
EPS = 1e-6
N_CORES = 8
SEQ_PER_CORE = 4
FF_GROUPS = [(0, 4), (4, 4), (8, 4), (12, 4), (16, 4), (20, 2)]
SLOT_ELEMS = 4096
N_SLOTS = 5
MASK_NEG = -30000.0


def tile_rows(j):
    return NMETA if j == 0 else 128


def tile_col0(j):
    return 0 if j == 0 else NMETA + (j - 1) * 128


TOK_GROUPS = [[0], [1, 2, 3, 4], [5, 6, 7, 8], [9, 10, 11, 12], [13, 14, 15, 16]]


def group_cols(g):
    tl = TOK_GROUPS[g]
    c0 = tile_col0(tl[0])
    c1 = tile_col0(tl[-1]) + tile_rows(tl[-1])
    return c0, c1


C_IDENT = 0
C_NEGTRI = 128
C_NEGONES = 256
C_MASK = 384
C_BLK = 512
C_G1 = 640
C_GM = 648
C_G2 = 656
C_GFIN = 664
C_CONVW = C_GFIN + 1024
C_GCONV = C_CONVW + 12
C_GATTN = C_GCONV + 4
C_BF = C_GATTN + 4
C_TOTAL = C_BF + 8


def make_consts(inp):
    c = np.zeros((128, C_TOTAL), np.float32)
    idx = np.arange(128)
    c[:, C_IDENT:C_IDENT + 128] = np.eye(128, dtype=np.float32)
    c[:, C_NEGTRI:C_NEGTRI + 128] = np.where(idx[:, None] <= idx[None, :], -1.0, 0.0)
    c[:, C_NEGONES:C_NEGONES + 128] = -1.0
    c[:, C_MASK:C_MASK + 128] = np.where(idx[:, None] <= idx[None, :], 0.0, MASK_NEG)
    c[:, C_BLK:C_BLK + 128] = np.where((idx[:, None] // 64) == (idx[None, :] // 64), 1.0 / 64, 0.0)
    c[:, C_G1:C_G1 + 8] = inp['ffn1_norm'].reshape(8, 128).T
    c[:, C_GM:C_GM + 8] = inp['mix_norm'].reshape(8, 128).T
    c[:, C_G2:C_G2 + 8] = inp['ffn2_norm'].reshape(8, 128).T
    c[:, C_GFIN:C_GFIN + 1024] = inp['final_norm'].reshape(1, 1024)
    cw = inp['conv_w'].reshape(3, 4, 128)
    c[:, C_CONVW:C_CONVW + 12] = cw.transpose(2, 1, 0).reshape(128, 12)
    c[:, C_GCONV:C_GCONV + 4] = inp['out_norm_conv'].reshape(4, 128).T
    c[:, C_GATTN:C_GATTN + 4] = inp['out_norm_attn'].reshape(4, 128).T
    c[:, C_BF:C_BF + 8] = inp['b_f'].reshape(1, 8)
    return np.ascontiguousarray(c)


def ffn_layout(w_gu, w_down):
    w_gu = w_gu.reshape(D, 2 * DFF)
    w_down = w_down.reshape(DFF, D)
    gu = w_gu.reshape(8, 128, 2 * DFF).transpose(1, 0, 2)
    dn = w_down.reshape(NCH, 128, D).transpose(1, 0, 2)
    pieces = []
    for (c0, n) in FF_GROUPS:
        pieces.append(gu[:, :, c0 * 128:(c0 + n) * 128].reshape(128, -1))
        pieces.append(gu[:, :, DFF + c0 * 128:DFF + (c0 + n) * 128].reshape(128, -1))
        pieces.append(dn[:, c0:c0 + n, :].reshape(128, -1))
    return np.ascontiguousarray(np.concatenate(pieces, axis=1))


def ffn_piece_offsets():
    offs = []
    o = 0
    for (c0, n) in FF_GROUPS:
        g = (o, 8 * n * 128); o += 8 * n * 128
        u = (o, 8 * n * 128); o += 8 * n * 128
        d = (o, n * 1024); o += n * 1024
        offs.append((g, u, d))
    return offs, o


MIX_FG = 64
MIX_UNIT = 8 * 3 * 128 + 1024


def mix_layout(w_in, w_out):
    w_in = w_in.reshape(D, 3080)
    w_out = w_out.reshape(D, D)
    wi = w_in.reshape(8, 128, 3080).transpose(1, 0, 2)
    wo = w_out.reshape(8, 128, D).transpose(1, 0, 2)
    pieces = [wi[:, :, 3072:3080].reshape(128, -1)]
    for cc in range(4):
        blk = np.stack([wi[:, :, cc * 128:(cc + 1) * 128],
                        wi[:, :, 512 + cc * 128:512 + (cc + 1) * 128],
                        wi[:, :, 1024 + cc * 128:1024 + (cc + 1) * 128]], axis=2)
        pieces.append(blk.reshape(128, -1))
        pieces.append(wo[:, cc, :])
    for p in range(4):
        blk = np.stack([wi[:, :, 1536 + p * 128:1536 + (p + 1) * 128],
                        wi[:, :, 2048 + p * 128:2048 + (p + 1) * 128],
                        wi[:, :, 2560 + p * 128:2560 + (p + 1) * 128]], axis=2)
        pieces.append(blk.reshape(128, -1))
        pieces.append(wo[:, 4 + p, :])
    return np.ascontiguousarray(np.concatenate(pieces, axis=1))


MIX_TOTAL = MIX_FG + 8 * MIX_UNIT


COMPUTE = ('pe', 'act', 'dve')


class Op:
    __slots__ = ('eng', 'fn', 'deps', 'dma', 'idx', 'sig', 'sigval', 'sem', 'prewait')

    def __init__(self, eng, fn, dma, idx):
        self.eng = eng; self.fn = fn; self.dma = dma; self.idx = idx
        self.deps = set(); self.sig = False; self.sigval = 0; self.sem = None; self.prewait = None


class Sched:
    def __init__(self):
        self.ops = []
        self.lastw = {}
        self.readers = {}

    def add(self, eng, fn, reads=(), writes=(), dma=False):
        idx = len(self.ops)
        op = Op(eng, fn, dma, idx)
        ops = self.ops
        wset = set(writes)
        for k in reads:
            w = self.lastw.get(k)
            if w is not None:
                op.deps.add(w)
        for k in wset:
            w = self.lastw.get(k)
            if w is not None:
                op.deps.add(w)
            for r in self.readers.get(k, {}).values():
                for ri in (r if isinstance(r, list) else [r]):
                    ro = ops[ri]
                    if ro.eng == eng and not ro.dma and not dma:
                        continue
                    op.deps.add(ri)
        for k in wset:
            self.lastw[k] = idx
            self.readers[k] = {}
        for k in reads:
            if k in wset:
                continue
            rd = self.readers.setdefault(k, {})
            if dma:
                rd.setdefault(('dma', eng), []).append(idx)
            else:
                rd[eng] = idx
        if eng == 'pe':
            op.deps = {d for d in op.deps if not (ops[d].eng == 'pe')}
        ops.append(op)
        return idx

    def lower(self, nc, sems, dma_sems):
        ops = self.ops
        for op in ops:
            for d in op.deps:
                ops[d].sig = True
        cnt = {e: 0 for e in sems}
        dcount = {}
        drr = {q: 0 for q in dma_sems}
        for op in ops:
            if op.dma:
                pool = dma_sems[op.eng]
                m = drr[op.eng] % len(pool)
                drr[op.eng] += 1
                sem = pool[m]
                prev = dcount.get((op.eng, m), 0)
                op.prewait = (sem, 16 * prev) if prev > 0 else None
                dcount[(op.eng, m)] = prev + 1
                op.sem = sem
                op.sigval = 16 * (prev + 1)
            elif op.sig:
                cnt[op.eng] += 1
                op.sem = sems[op.eng]
                op.sigval = cnt[op.eng]
        streams = {}
        waited = {}
        for op in ops:
            st = streams.setdefault(op.eng, [])
            need = {}
            if op.prewait is not None:
                need[id(op.prewait[0])] = op.prewait
            for d in op.deps:
                do = ops[d]
                key = id(do.sem)
                if key not in need or need[key][1] < do.sigval:
                    need[key] = (do.sem, do.sigval)
            waits = []
            for key, (sem, val) in need.items():
                wk = (op.eng, key)
                if waited.get(wk, 0) >= val:
                    continue
                waited[wk] = val
                waits.append((sem, val))
            sig = None
            if op.dma:
                sig = (op.sem, 16)
            elif op.sig:
                sig = (op.sem, 1)
            st.append((waits, op, sig))
        finals = {}
        for (q, m), c in dcount.items():
            finals.setdefault(q, []).append((dma_sems[q][m], 16 * c))
        return streams, finals


class Rot:
    def __init__(self, n):
        self.n = n; self.i = -1

    def next(self):
        self.i = (self.i + 1) % self.n
        return self.i


def build_program(n_seq=SEQ_PER_CORE, stop_after=None):
    nc = bass.Bass("TRN2", target_bir_lowering=False)
    ffn_offs, ffn_total = ffn_piece_offsets()
    x_d = nc.dram_tensor("x", [n_seq, SEQ, D], F32, kind="ExternalInput").ap()
    meta_d = nc.dram_tensor("meta", [NMETA, D], F32, kind="ExternalInput").ap()
    cst_d = nc.dram_tensor("cst", [128, C_TOTAL], F32, kind="ExternalInput").ap()
    f1_d = nc.dram_tensor("ffn1", [128, ffn_total], F32, kind="ExternalInput").ap()
    f2_d = nc.dram_tensor("ffn2", [128, ffn_total], F32, kind="ExternalInput").ap()
    mx_d = nc.dram_tensor("mixw", [128, MIX_TOTAL], F32, kind="ExternalInput").ap()
    y_d = nc.dram_tensor("y", [n_seq, SEQ, D], F32, kind="ExternalOutput").ap()
    if stop_after == 'mix':
        dbg_d = nc.dram_tensor("dbg", [128, 1024], F32, kind="ExternalOutput").ap()
        dbg2_d = nc.dram_tensor("dbg2", [128, NT * 128 + L], F32, kind="ExternalOutput").ap()

    S = Sched()
    from contextlib import ExitStack
    es = ExitStack()

    def sb(name, shape, dt):
        return es.enter_context(nc.sbuf_tensor(name, shape, dt))

    with es:
        h = sb("h", [128, NT, D], F32)
        xT = sb("xT", [128, 8, L], BF16)
        cst = sb("cst_sb", [128, C_TOTAL], F32)
        ident = sb("ident", [128, 128], BF16)
        maskb = sb("maskb", [128, 128], BF16)
        zeros = sb("zeros", [128, 264], BF16)
        ring = sb("ring", [128, N_SLOTS, SLOT_ELEMS], BF16)
        ss = sb("ss", [128, NT], F32)
        lnv = sb("lnv", [128, NT], F32)
        rstd = sb("rstd", [128, NT], F32)
        fence = sb("fence", [128, 8], F32)
        xs = sb("xs", [128, 2, D], BF16)
        sg = sb("sg", [128, 2, 512], F32)
        actT = sb("actT", [128, 2, 4, 512], BF16)
        zt = sb("zt", [128, NT, 8], F32)
        Cc = sb("Cc", [128, NT + 1, 8], F32)
        Ft = sb("Ft", [128, NT, 8], F32)
        NPAIR = 9
        Rp = sb("Rp", [128, NPAIR, 8], F32)
        biast = sb("biast", [128, 2, NPAIR, NT], F32)
        qT = sb("qT", [128, L], BF16)
        kA = sb("kA", [128, L], BF16)
        kB = sb("kB", [128, L], BF16)
        vaug = sb("vaug", [128, NT, 2, 65], BF16)
        ub = sb("ub", [128, 514], F32)
        cS = sb("cS", [128, 512], F32)
        acc = sb("acc", [128, 512], F32)
        yc = sb("yc", [128, 512], F32)
        yT = sb("yT", [128, 2, L], BF16)
        pT = sb("pT", [128, 3, 512], BF16)
        junk = pT[:, 0:2, :].rearrange("p a b -> p (a b)")
        lsp = zt
        sqo = cS[:, 0:256].rearrange("p (a b) -> p a b", b=64)
        yv = cS[:, 256:512].rearrange("p (a b) -> p a b", b=64)
        ysq = cS
        lnr = acc
        rr = acc
        ytok = sb("ytok", [128, NT, 128], BF16)
        sso = sb("sso", [128, 4], F32)
        zc = sb("zc", [128, 4], F32)
        z2 = sb("z2", [128, 4], F32)
        lnt = sb("lnt", [128, 4], F32)
        ro = sb("ro", [128, 4], F32)

        psA = [es.enter_context(nc.psum_tensor(f"psA{i}", [128, 512], F32)) for i in range(4)]
        psB = [es.enter_context(nc.psum_tensor(f"psB{i}", [128, 1024], F32)) for i in range(2)]

        def bank(b):
            if b < 4:
                return psA[b][:, :]
            return psB[(b - 4) // 2][:, ((b - 4) % 2) * 512:((b - 4) % 2) * 512 + 512]

        rotA = Rot(4)
        rotB = Rot(2)
        rotO = Rot(2)
        rot_xs = Rot(2); rot_sg = Rot(2); rot_act = Rot(2); rot_pT = Rot(3)
        ring_rot = Rot(N_SLOTS)

        def cc(col, n=1):
            return cst[:, col:col + n]

        def A(fn, reads, writes):
            S.add('act', fn, reads, writes)

        def V(fn, reads, writes):
            S.add('dve', fn, reads, writes)

        def P(fn, reads, writes):
            S.add('pe', fn, reads, writes)

        def act_fence(reads):
            A(lambda e: e.activation(out=fence[:, 0:1], in_=fence[:, 1:2], func=AF.Copy),
              list(reads) + ['fence_in'], ['fence_out'])

        S.add('sp', lambda e: e.dma_start(out=cst[:, :], in_=cst_d[:, :]), [], ['cst'], dma=True)
        V(lambda e: e.tensor_copy(out=ident[:, :], in_=cst[:, C_IDENT:C_IDENT + 128]), ['cst'], ['ident'])
        V(lambda e: e.tensor_copy(out=maskb[:, :], in_=cst[:, C_MASK:C_MASK + 128]), ['cst'], ['maskb'])
        V(lambda e: e.memset(zeros[:, :], 0.0), [], ['zeros'])
        V(lambda e: e.memset(fence[:, :], 0.0), [], ['fence_in', 'fence_out'])
        V(lambda e: e.memset(ss[:, :], 1.0), [], [('ss', g_) for g_ in range(5)])
        V(lambda e: e.memset(zt[:, :, :], 0.0), [], ['zt'])
        V(lambda e: e.memset(Cc[:, :, :], 0.0), [], ['Cc'])
        V(lambda e: e.memset(ub[:, :], 0.0), [], ['ub0'])
        V(lambda e: e.memset(vaug[:, :, :, :], 1.0), [], [('vaug', j_) for j_ in range(NT)])
        V(lambda e: e.memset(h[:, 0, :], 0.0), [], [('h', 0)])
        V(lambda e: e.memset(ytok[:, :, :], 0.0), [], ['ytok'])
        V(lambda e: e.memset(Ft[:, :, :], 0.0), [], ['Ft'])
        V(lambda e: e.memset(lnv[:, :], 0.0), [], [('lnv', g_) for g_ in range(5)])
        V(lambda e: e.memset(rstd[:, :], 1.0), [], [('rstd', g_) for g_ in range(5)])
        V(lambda e: e.memset(kA[:, :], 0.0), [], [('kT', g_) for g_ in range(5)])
        V(lambda e: e.memset(kB[:, :], 0.0), [], [('kT', g_) for g_ in range(5)])

        def load_piece(src, off, n):
            slot = ring_rot.next()
            assert n <= SLOT_ELEMS
            S.add('pool', lambda e: e.dma_start(out=ring[:, slot, 0:n], in_=src[:, off:off + n],
                                                max_dma_last_dim=8192),
                  [], [('ring', slot)], dma=True)
            return slot

        def norm_front(g, gcol):
            tl = TOK_GROUPS[g]
            j0, j1 = tl[0], tl[-1] + 1
            for j in tl:
                R = tile_rows(j)
                A(lambda e, j=j, R=R: e.activation(out=junk[:R, :], in_=h[:R, j, :], func=AF.Square,
                                                   accum_out=ss[:R, j:j + 1]),
                  [('h', j)], [('pT', 0), ('pT', 1), ('ss', g)])
            act_fence([('ss', g)])
            A(lambda e: e.activation(out=lnv[:, j0:j1], in_=ss[:, j0:j1], func=AF.Ln, scale=1.0 / D, bias=epst[:, 0:1]),
              [('ss', g), 'fence_out', 'epst'], [('lnv', g)])
            A(lambda e: e.activation(out=rstd[:, j0:j1], in_=lnv[:, j0:j1], func=AF.Exp, scale=-0.5), [('lnv', g)], [('rstd', g)])

            def back():
                for j in tl:
                    R = tile_rows(j)
                    c0 = tile_col0(j)
                    xb = rot_xs.next()
                    if j % 2 == 0:
                        A(lambda e, j=j, R=R, xb=xb: e.activation(out=xs[:R, xb, :], in_=h[:R, j, :], func=AF.Copy,
                                                                  scale=rstd[:R, j:j + 1]),
                          [('h', j), ('rstd', g)], [('xs', xb)])
                    else:
                        V(lambda e, j=j, R=R, xb=xb: e.tensor_scalar(out=xs[:R, xb, :], in0=h[:R, j, :], scalar1=rstd[:R, j:j + 1],
                                                                     scalar2=None, op0=ALU.mult),
                          [('h', j), ('rstd', g)], [('xs', xb)])
                    b = rotA.next()
                    tp = bank(b).bitcast(BF16)
                    tp3 = tp.rearrange('p (k t) -> p k t', t=128)
                    for k in range(8):
                        P(lambda e, k=k, R=R, xb=xb, tp3=tp3: e.transpose(out=tp3[:, k, :R],
                                                                        in_=xs[:R, xb, k * 128:(k + 1) * 128],
                                                                        identity=ident[:R, :R]),
                          [('xs', xb), 'ident'], [('ps', b)])
                    gbc = cst[:, gcol:gcol + 8].unsqueeze(2).to_broadcast([128, 8, R])
                    V(lambda e, R=R, c0=c0, tp3=tp3, gbc=gbc: e.tensor_tensor(out=xT[:, :, c0:c0 + R], in0=tp3[:, :, :R],
                                                                              in1=gbc, op=ALU.mult),
                      [('ps', b), 'cst'], [('xT', j)])
            return back

        def norm_to_xT(gcol):
            for g in range(len(TOK_GROUPS)):
                norm_front(g, gcol)()

        epst = sb("epst", [128, 1], F32)
        V(lambda e: e.memset(epst[:, :], EPS), [], ['epst'])
        onet = sb("onet", [128, 1], F32)
        V(lambda e: e.memset(onet[:, :], 1.0), [], ['onet'])

        def ffn(src_d, tail=None, head=None):
            for gi, (c0, n) in enumerate(FF_GROUPS):
                last_grp = (gi == len(FF_GROUPS) - 1)
                deferred = None
                (go, gl), (uo, ul), (do_, dl) = ffn_offs[gi]
                sg_slot = load_piece(src_d, go, gl)
                su_slot = load_piece(src_d, uo, ul)
                sd_slot = load_piece(src_d, do_, dl)
                wg = ring[:, sg_slot, 0:gl].rearrange('p (k c) -> p k c', k=8)
                wu = ring[:, su_slot, 0:ul].rearrange('p (k c) -> p k c', k=8)
                wd = ring[:, sd_slot, 0:dl].rearrange('p (i c) -> p i c', c=D)
                for g in range(len(TOK_GROUPS)):
                    t0, t1_ = group_cols(g)
                    T = t1_ - t0
                    xkeys = [('xT', j) for j in TOK_GROUPS[g]]
                    ab = rot_act.next()
                    for i in range(n):
                        bg = rotA.next(); bu = rotA.next()
                        for k in range(8):
                            P(lambda e, i=i, k=k, bg=bg, wg=wg, t0=t0, T=T: e.matmul(
                                out=bank(bg)[:, :T], lhsT=wg[:, k, i * 128:(i + 1) * 128], rhs=xT[:, k, t0:t0 + T],
                                start=(k == 0), stop=(k == 7)),
                              xkeys + [('ring', sg_slot)], [('ps', bg)])
                        for k in range(8):
                            P(lambda e, i=i, k=k, bu=bu, wu=wu, t0=t0, T=T: e.matmul(
                                out=bank(bu)[:, :T], lhsT=wu[:, k, i * 128:(i + 1) * 128], rhs=xT[:, k, t0:t0 + T],
                                start=(k == 0), stop=(k == 7)),
                              xkeys + [('ring', su_slot)], [('ps', bu)])
                        sb_ = rot_sg.next()
                        A(lambda e, bg=bg, sb_=sb_, T=T: e.activation(out=sg[:, sb_, :T], in_=bank(bg)[:, :T], func=AF.Silu),
                          [('ps', bg)], [('sg', sb_)])
                        V(lambda e, bu=bu, sb_=sb_, ab=ab, i=i, T=T: e.tensor_tensor(
                            out=actT[:, ab, i, :T], in0=bank(bu)[:, :T], in1=sg[:, sb_, :T], op=ALU.mult),
                          [('ps', bu), ('sg', sb_)], [('actT', ab, i)])
                    if deferred is not None:
                        deferred(); deferred = None
                    if gi == 0 and head is not None and g + 1 < len(TOK_GROUPS):
                        head(g + 1)
                    for j in TOK_GROUPS[g]:
                        R = tile_rows(j)
                        lc = tile_col0(j) - t0
                        pb = rotB.next()
                        dps = psB[pb]
                        for half in range(2):
                            for i in range(n):
                                P(lambda e, i=i, half=half, dps=dps, ab=ab, lc=lc, R=R, wd=wd, n=n: e.matmul(
                                    out=dps[:R, half * 512:(half + 1) * 512], lhsT=actT[:, ab, i, lc:lc + R],
                                    rhs=wd[:, i, half * 512:(half + 1) * 512], start=(i == 0), stop=(i == n - 1)),
                                  [('actT', ab, i), ('ring', sd_slot)], [('ps', 4 + 2 * pb + half)])
                        V(lambda e, dps=dps, R=R, j=j: e.scalar_tensor_tensor(
                            out=h[:R, j, :], in0=dps[:R, :], scalar=0.5, in1=h[:R, j, :], op0=ALU.mult, op1=ALU.add),
                          [('ps', 4 + 2 * pb), ('ps', 5 + 2 * pb), ('h', j)], [('h', j)])
                    if last_grp and tail is not None:
                        deferred = tail(g)
                if deferred is not None:
                    deferred(); deferred = None

        def wout_partial(wo_aps, wo_keys, tail=None):
            deferred = None
            for j in range(NT):
                R = tile_rows(j)
                c0 = tile_col0(j)
                pb = rotB.next()
                dps = psB[pb]
                for half in range(2):
                    for s_ in range(2):
                        P(lambda e, s_=s_, half=half, dps=dps, c0=c0, R=R: e.matmul(
                            out=dps[:R, half * 512:(half + 1) * 512], lhsT=yT[:, s_, c0:c0 + R],
                            rhs=wo_aps[s_][:, half * 512:(half + 1) * 512], start=(s_ == 0), stop=(s_ == 1)),
                          [('yT', s_), wo_keys[s_]], [('ps', 4 + 2 * pb + half)])
                V(lambda e, dps=dps, R=R, j=j: e.tensor_tensor(out=h[:R, j, :], in0=dps[:R, :], in1=h[:R, j, :], op=ALU.add),
                  [('ps', 4 + 2 * pb), ('ps', 5 + 2 * pb), ('h', j)], [('h', j)])
                gdone = [g_ for g_, tl_ in enumerate(TOK_GROUPS) if tl_[-1] == j]
                if gdone:
                    if deferred is not None:
                        deferred(); deferred = None
                    if tail is not None:
                        deferred = tail(gdone[0])
            if deferred is not None:
                deferred()

        def wout_items(wo_aps, wo_keys):
            items = []
            for j in range(NT):
                def it(j=j):
                    R = tile_rows(j); c0 = tile_col0(j)
                    for half in range(2):
                        b = rotA.next()
                        for s_ in range(2):
                            P(lambda e, s_=s_, half=half, b=b: e.matmul(
                                out=bank(b)[:R, :], lhsT=yT[:, s_, c0:c0 + R],
                                rhs=wo_aps[s_][:, half * 512:(half + 1) * 512], start=(s_ == 0), stop=(s_ == 1)),
                              [('yT', s_), wo_keys[s_]], [('ps', b)])
                        V(lambda e, half=half, b=b: e.tensor_tensor(out=h[:R, j, half * 512:(half + 1) * 512], in0=bank(b)[:R, :],
                                                                    in1=h[:R, j, half * 512:(half + 1) * 512], op=ALU.add),
                          [('ps', b), ('h', j)], [('h', j)])
                items.append(it)
            return items

        def mixer(tail=None):
            sfg = load_piece(mx_d, 0, MIX_FG)
            wfg = ring[:, sfg, 0:MIX_FG].rearrange('p (k c) -> p k c', k=8)
            for j in range(NT):
                R = tile_rows(j); c0 = tile_col0(j)
                b = rotA.next()
                for k in range(8):
                    P(lambda e, k=k, b=b, R=R, c0=c0: e.matmul(out=bank(b)[:R, 0:8], lhsT=xT[:, k, c0:c0 + R],
                                                              rhs=wfg[:, k, :], start=(k == 0), stop=(k == 7)),
                      [('xT', j), ('ring', sfg)], [('ps', b)])
                V(lambda e, b=b, R=R, j=j: e.tensor_tensor(out=zt[:R, j, :], in0=bank(b)[:R, 0:8],
                                                           in1=cst[:R, C_BF:C_BF + 8], op=ALU.add),
                  [('ps', b), 'cst'], ['zt'])
            A(lambda e: e.activation(out=zt[:, :, :], in_=zt[:, :, :], func=AF.Exp, scale=-1.0), ['zt'], ['zt'])
            A(lambda e: e.activation(out=zt[:, :, :], in_=zt[:, :, :], func=AF.Ln, bias=onet[:, 0:1]), ['zt', 'onet'], ['zt', 'lsp'])
            for j in range(NT):
                R = tile_rows(j)
                b1 = rotA.next(); b2 = rotA.next()
                P(lambda e, b1=b1, R=R, j=j: e.matmul(out=bank(b1)[:R, 0:8], lhsT=cst[:R, C_NEGTRI:C_NEGTRI + R],
                                                      rhs=lsp[:R, j, :], start=True, stop=True),
                  ['lsp', 'cst'], [('ps', b1)])
                P(lambda e, b2=b2, R=R, j=j: e.matmul(out=bank(b2)[:, 0:8], lhsT=cst[:R, C_NEGONES:C_NEGONES + 128],
                                                      rhs=lsp[:R, j, :], start=True, stop=True),
                  ['lsp', 'cst'], [('ps', b2)])
                V(lambda e, b1=b1, R=R, j=j: e.tensor_tensor(out=Ft[:R, j, :], in0=bank(b1)[:R, 0:8], in1=Cc[:R, j, :], op=ALU.add),
                  [('ps', b1), 'Cc'], ['Ft'])
                V(lambda e, b2=b2, j=j: e.tensor_tensor(out=Cc[:, j + 1, :], in0=bank(b2)[:, 0:8], in1=Cc[:, j, :], op=ALU.add),
                  [('ps', b2), 'Cc'], ['Cc'])
            V(lambda e: e.tensor_copy(out=Rp[:, 0, :], in_=Cc[:, 1, :]), ['Cc'], ['Rp'])
            V(lambda e: e.tensor_copy(out=Rp[:, 1:NPAIR, :], in_=Cc[:, 2:NT + 1, :].rearrange('p (a two) h -> p a two h', two=2)[:, :, 0, :]),
              ['Cc'], ['Rp'])
            pending_fin = []

            ALIAS_KEYS = [('actT', a_, i_) for a_ in range(2) for i_ in range(4)] + [('sg', 0), ('sg', 1)]
            SET_KEYS = [['ub0', 'cS', 'acc', 'yc'], ['ub1', 'cS1', 'acc1', 'yc1']]
            V(lambda e: e.memset(fence[:, 2:3], 0.0), [], ALIAS_KEYS + SET_KEYS[1])
            actF = actT[:, :, :, :].rearrange('p a b c -> p (a b c)').bitcast(F32)
            sgF = sg[:, :, :].rearrange('p a b -> p (a b)')
            ubs = [ub, actF[:, 0:514]]
            cSs = [cS, actF[:, 514:1026]]
            accs = [acc, actF[:, 1026:1538]]
            ycs = [yc, sgF[:, 0:512]]
            it = 0
            pend_back = None
            for pair in range(2):
                wo_aps = []; wo_keys = []
                for s_ in range(2):
                    cch = pair * 2 + s_
                    off = MIX_FG + cch * MIX_UNIT
                    sw = load_piece(mx_d, off, MIX_UNIT)
                    wcv = ring[:, sw, 0:3072].rearrange('p (k t c) -> p k t c', k=8, t=3)
                    wo_aps.append(ring[:, sw, 3072:4096]); wo_keys.append(('ring', sw))
                    for g in range(len(TOK_GROUPS)):
                        t0, t1_ = group_cols(g); T = t1_ - t0
                        xkeys = [('xT', j) for j in TOK_GROUPS[g]]
                        p = it % 2; it += 1
                        kub, kcS, kacc, kyc = SET_KEYS[p]
                        ub_, cS_, acc_, yc_ = ubs[p], cSs[p], accs[p], ycs[p]
                        ubprev, kubprev = ubs[1 - p], SET_KEYS[1 - p][0]
                        bc_ = rotA.next(); bh_ = rotA.next(); bb_ = 4 + p; bst = 6 + p
                        for which, bk in ((1, bc_), (2, bh_), (0, bb_)):
                            for k in range(8):
                                P(lambda e, k=k, bk=bk, which=which, wcv=wcv, t0=t0, T=T: e.matmul(
                                    out=bank(bk)[:, :T], lhsT=wcv[:, k, which, :], rhs=xT[:, k, t0:t0 + T],
                                    start=(k == 0), stop=(k == 7)),
                                  xkeys + [('ring', sw)], [('ps', bk)])
                        A(lambda e, bc_=bc_, T=T, cS_=cS_: e.activation(out=cS_[:, :T], in_=bank(bc_)[:, :T], func=AF.Copy),
                          [('ps', bc_)], [kcS])
                        if g == 0:
                            V(lambda e, ub_=ub_: e.memset(ub_[:, 0:2], 0.0), [], [kub])
                        else:
                            V(lambda e, Tp=Tprev, ub_=ub_, ubprev=ubprev: e.tensor_copy(out=ub_[:, 0:2], in_=ubprev[:, Tp:Tp + 2]),
                              [kubprev], [kub])
                        Tprev = T
                        V(lambda e, bh_=bh_, T=T, ub_=ub_, cS_=cS_: e.tensor_tensor(out=ub_[:, 2:2 + T], in0=bank(bh_)[:, :T],
                                                                                  in1=cS_[:, :T], op=ALU.mult),
                          [('ps', bh_), kcS], [kub])
                        w0 = cst[:, C_CONVW + cch * 3 + 0:C_CONVW + cch * 3 + 1]
                        w1 = cst[:, C_CONVW + cch * 3 + 1:C_CONVW + cch * 3 + 2]
                        w2 = cst[:, C_CONVW + cch * 3 + 2:C_CONVW + cch * 3 + 3]
                        V(lambda e, T=T, w2=w2, ub_=ub_, acc_=acc_: e.tensor_scalar(out=acc_[:, :T], in0=ub_[:, 2:2 + T], scalar1=w2,
                                                                                 scalar2=None, op0=ALU.mult), [kub, 'cst'], [kacc])
                        V(lambda e, T=T, w1=w1, ub_=ub_, acc_=acc_: e.scalar_tensor_tensor(out=acc_[:, :T], in0=ub_[:, 1:1 + T], scalar=w1,
                                                                                        in1=acc_[:, :T], op0=ALU.mult, op1=ALU.add),
                          [kub, 'cst', kacc], [kacc])
                        V(lambda e, T=T, w0=w0, ub_=ub_, acc_=acc_: e.scalar_tensor_tensor(out=acc_[:, :T], in0=ub_[:, 0:T], scalar=w0,
                                                                                        in1=acc_[:, :T], op0=ALU.mult, op1=ALU.add),
                          [kub, 'cst', kacc], [kacc])
                        V(lambda e, bb_=bb_, T=T, yc_=yc_, acc_=acc_: e.tensor_tensor(out=yc_[:, :T], in0=bank(bb_)[:, :T], in1=acc_[:, :T], op=ALU.mult),
                          [('ps', bb_), kacc], [kyc])
                        A(lambda e, T=T, yc_=yc_, cS_=cS_: e.activation(out=cS_[:, :T], in_=yc_[:, :T], func=AF.Square), [kyc], [kcS])

                        def back(T=T, t0=t0, s_=s_, cch=cch, bst=bst, cS_=cS_, acc_=acc_, yc_=yc_, kcS=kcS, kacc=kacc, kyc=kyc):
                            P(lambda e: e.matmul(out=bank(bst)[:, :T], lhsT=cst[:, C_BLK:C_BLK + 128], rhs=cS_[:, :T],
                                                 start=True, stop=True), [kcS, 'cst'], [('ps', bst)])
                            A(lambda e: e.activation(out=acc_[:, :T], in_=bank(bst)[:, :T], func=AF.Ln, bias=epst[:, 0:1]),
                              [('ps', bst), 'epst'], [kacc])
                            A(lambda e: e.activation(out=acc_[:, :T], in_=acc_[:, :T], func=AF.Exp, scale=-0.5), [kacc], [kacc])
                            gcv = cst[:, C_GCONV + cch:C_GCONV + cch + 1]
                            V(lambda e: e.scalar_tensor_tensor(
                                out=yT[:, s_, t0:t0 + T], in0=yc_[:, :T], scalar=gcv, in1=acc_[:, :T], op0=ALU.mult, op1=ALU.mult),
                              [kyc, kacc, 'cst'], [('yT', s_)])
                        if pend_back is not None:
                            pend_back()
                        pend_back = back
                if pend_back is not None:
                    pend_back(); pend_back = None
                if pair == 0:
                    wout_partial(wo_aps, wo_keys)
                else:
                    bg_work = wout_items(wo_aps, wo_keys)
            V(lambda e: e.memset(fence[:, 3:4], 0.0), [], ALIAS_KEYS + SET_KEYS[1])

            for pair in range(2):
                wo_aps = []; wo_keys = []
                for s_ in range(2):
                    hp = pair * 2 + s_
                    off = MIX_FG + (4 + hp) * MIX_UNIT
                    sw = load_piece(mx_d, off, MIX_UNIT)
                    wqkv = ring[:, sw, 0:3072].rearrange('p (k t c) -> p k t c', k=8, t=3)
                    wo_aps.append(ring[:, sw, 3072:4096]); wo_keys.append(('ring', sw))
                    def proj_items(g, wqkv=wqkv, sw=sw):
                        items = []
                        t0, t1_ = group_cols(g); T = t1_ - t0
                        xkeys = [('xT', j) for j in TOK_GROUPS[g]]
                        for which, bkk in ((0, 6), (1, 7)):
                            for k in range(8):
                                def it(k=k, bkk=bkk, which=which):
                                    P(lambda e: e.matmul(
                                        out=bank(bkk)[:, :T], lhsT=wqkv[:, k, which, :], rhs=xT[:, k, t0:t0 + T],
                                        start=(k == 0), stop=(k == 7)),
                                      xkeys + [('ring', sw)], [('ps', bkk)])
                                    if k == 7 and which == 0:
                                        V(lambda e: e.tensor_scalar(out=qT[:, t0:t0 + T], in0=bank(6)[:, :T], scalar1=0.125,
                                                                    scalar2=None, op0=ALU.mult),
                                          [('ps', 6)], [('qT', g)])
                                    if k == 7 and which == 1:
                                        V(lambda e: e.tensor_copy(out=kA[0:64, t0:t0 + T], in_=bank(7)[0:64, :T]), [('ps', 7)], [('kT', g)])
                                        V(lambda e: e.tensor_copy(out=kB[64:128, t0:t0 + T], in_=bank(7)[64:128, :T]), [('ps', 7)], [('kT', g)])
                                items.append(it)
                        for n_, j in enumerate(TOK_GROUPS[g]):
                            R = tile_rows(j); c0 = tile_col0(j)
                            bv = 6 + (n_ % 2)
                            for k in range(8):
                                def it(k=k, bv=bv, R=R, c0=c0, j=j):
                                    P(lambda e: e.matmul(out=bank(bv)[:R, 0:128], lhsT=xT[:, k, c0:c0 + R],
                                                         rhs=wqkv[:, k, 2, :], start=(k == 0), stop=(k == 7)),
                                      [('xT', j), ('ring', sw)], [('ps', bv)])
                                    if k == 7:
                                        V(lambda e: e.tensor_copy(out=vaug[:R, j, :, 0:64],
                                                                  in_=bank(bv)[:R, 0:128].rearrange('p (a d) -> p a d', a=2)),
                                          [('ps', bv)], [('vaug', j)])
                                items.append(it)
                        return items

                    for hh in range(2):
                        hd = hp * 2 + hh
                        rm = Rp[:, :, hd].unsqueeze(2).to_broadcast([128, NPAIR, NT])
                        fk = Ft[:, :, hd].unsqueeze(1).to_broadcast([128, NPAIR, NT])
                        V(lambda e, hh=hh, rm=rm, fk=fk: e.tensor_tensor(out=biast[:, hh, :, :], in0=rm, in1=fk, op=ALU.subtract),
                          ['Rp', 'Ft'], [('biast', hh)])
                    for it_ in proj_items(0):
                        it_()
                    pend_pv = None
                    for g in range(len(TOK_GROUPS)):
                        work = proj_items(g + 1) if g + 1 < len(TOK_GROUPS) else []
                        n_iter = 2 * (TOK_GROUPS[g][-1] + 1)
                        per_it = -(-len(work) // max(1, n_iter - 2))
                        for hh in range(2):
                            blist = TOK_GROUPS[g]
                            nb = len(blist)
                            ob = 4 + rotO.next()
                            op3 = bank(ob)[:, 0:nb * 65].rearrange('p (b d) -> p b d', d=65)
                            P(lambda e, ob=ob, nb=nb: e.matmul(out=bank(ob)[:, 0:nb * 65], lhsT=zeros[0:1, 0:128], rhs=zeros[0:1, 0:nb * 65],
                                                               start=True, stop=True),
                              ['zeros'], [('ps', ob)])

                            def emit_st(j, blist=blist, g=g, hh=hh):
                                Rk = tile_rows(j); kc0 = tile_col0(j)
                                gj = [g_ for g_, tl_ in enumerate(TOK_GROUPS) if j in tl_][0]
                                bl = [b for b in blist if b >= j]
                                q0 = tile_col0(bl[0]); q1 = tile_col0(bl[-1]) + tile_rows(bl[-1])
                                Tq = q1 - q0
                                bs_ = rotA.next()
                                diag = (j in blist)
                                P(lambda e: e.matmul(
                                    out=bank(bs_)[:Rk, :Tq], lhsT=(kA if hh == 0 else kB)[:, kc0:kc0 + Rk], rhs=qT[:, q0:q0 + Tq],
                                    start=True, stop=(not diag)),
                                  [('kT', gj), ('qT', g)], [('ps', bs_)])
                                if diag:
                                    P(lambda e: e.matmul(out=bank(bs_)[:Rk, :Rk], lhsT=ident[:Rk, :Rk], rhs=maskb[:Rk, :Rk],
                                                         start=False, stop=True),
                                      ['ident', 'maskb'], [('ps', bs_)])
                                pb_ = rot_pT.next()
                                pieces = {}
                                for b in bl:
                                    pi = 0 if g == 0 else 2 * (g - 1) + 1 + blist.index(b) // 2
                                    pieces.setdefault(pi, []).append(b)
                                for pi, bs in pieces.items():
                                    lc0 = tile_col0(bs[0]) - q0
                                    lc1 = tile_col0(bs[-1]) + tile_rows(bs[-1]) - q0
                                    A(lambda e, lc0=lc0, lc1=lc1, pi=pi: e.activation(
                                        out=pT[:Rk, pb_, lc0:lc1], in_=bank(bs_)[:Rk, lc0:lc1], func=AF.Exp,
                                        bias=biast[:Rk, hh, pi, j:j + 1]),
                                      [('ps', bs_), ('biast', hh)], [('pT', pb_)])
                                return (j, Rk, bl, q0, pb_)

                            def emit_pv(st, blist=blist, op3=op3, ob=ob, hh=hh):
                                j, Rk, bl, q0, pb_ = st
                                for b in bl:
                                    Rb = tile_rows(b); lc = tile_col0(b) - q0; bi = blist.index(b)
                                    P(lambda e, Rb=Rb, lc=lc, bi=bi, b=b: e.matmul(
                                        out=op3[:Rb, bi, :], lhsT=pT[:Rk, pb_, lc:lc + Rb], rhs=vaug[:Rk, j, hh, :],
                                        start=False, stop=(j == b), skip_group_check=True),
                                      [('pT', pb_), ('vaug', j)], [('ps', ob)])

                            for j in range(blist[-1] + 1):
                                cur = emit_st(j)
                                if pend_pv is not None:
                                    pend_pv()
                                pend_pv = (lambda cur=cur, emit_pv=emit_pv: emit_pv(cur))
                                if pending_fin and (j == 1 or j == blist[-1]):
                                    pending_fin.pop()()
                                for _ in range(per_it):
                                    if work:
                                        work.pop(0)()
                                if bg_work and j >= 1:
                                    bg_work.pop(0)()

                            def fin(blist=blist, nb=nb, op3=op3, ob=ob, hh=hh):
                                Rg = tile_rows(blist[0])
                                V(lambda e: e.tensor_copy(out=zc[:Rg, :nb], in_=op3[:Rg, :, 64]),
                                  [('ps', ob)], ['zc'])
                                V(lambda e: e.reciprocal(out=z2[:Rg, :nb], in_=zc[:Rg, :nb]), ['zc'], ['z2'])
                                for bi, b in enumerate(blist):
                                    V(lambda e, bi=bi: e.tensor_scalar(
                                        out=yv[:Rg, bi, :], in0=op3[:Rg, bi, 0:64], scalar1=z2[:Rg, bi:bi + 1],
                                        scalar2=None, op0=ALU.mult),
                                      [('ps', ob), 'z2'], ['cS'])
                                V(lambda e: e.tensor_tensor(out=sqo[:Rg, :nb, :], in0=yv[:Rg, :nb, :], in1=yv[:Rg, :nb, :], op=ALU.mult),
                                  ['cS'], ['cS'])
                                V(lambda e: e.reduce_sum(out=sso[:Rg, :nb], in_=sqo[:Rg, :nb, :], axis=AX.X), ['cS'], ['sso'])
                                A(lambda e: e.activation(out=lnt[:Rg, :nb], in_=sso[:Rg, :nb], func=AF.Ln, scale=1.0 / 64,
                                                         bias=epst[:Rg, 0:1]), ['sso', 'epst'], ['lnt'])
                                A(lambda e: e.activation(out=ro[:Rg, :nb], in_=lnt[:Rg, :nb], func=AF.Exp, scale=-0.5), ['lnt'], ['ro'])
                                for bi, b in enumerate(blist):
                                    V(lambda e, bi=bi, b=b: e.tensor_scalar(
                                        out=ytok[:Rg, b, hh * 64:(hh + 1) * 64], in0=yv[:Rg, bi, :], scalar1=ro[:Rg, bi:bi + 1],
                                        scalar2=None, op0=ALU.mult),
                                      ['cS', 'ro'], [('ytok', b)])
                            pending_fin.append(fin)
                        while work:
                            work.pop(0)()
                    if pend_pv is not None:
                        pend_pv(); pend_pv = None
                    if pending_fin:
                        pending_fin.pop()()
                    while bg_work:
                        bg_work.pop(0)()
                    gat = cst[:, C_GATTN + hp:C_GATTN + hp + 1]
                    for j in range(NT):
                        R = tile_rows(j); c0 = tile_col0(j)
                        b = rotA.next()
                        tpv = bank(b).bitcast(BF16)
                        P(lambda e, tpv=tpv, R=R, j=j: e.transpose(out=tpv[:, 0:R], in_=ytok[:R, j, :], identity=ident[:R, :R]),
                          [('ytok', j), 'ident'], [('ps', b)])
                        V(lambda e, tpv=tpv, R=R, c0=c0, s_=s_, gat=gat: e.tensor_scalar(out=yT[:, s_, c0:c0 + R], in0=tpv[:, 0:R], scalar1=gat,
                                                                                       scalar2=None, op0=ALU.mult),
                          [('ps', b), 'cst'], [('yT', s_)])
                if pair == 1:
                    wout_partial(wo_aps, wo_keys, tail=tail)
                else:
                    bg_work = wout_items(wo_aps, wo_keys)

        def final_store(s, do_norm=True):
            if do_norm:
                for j in range(1, NT):
                    A(lambda e, j=j: e.activation(out=junk[:, :], in_=h[:, j, :], func=AF.Square, accum_out=ss[:, j:j + 1]),
                      [('h', j)], [('pT', 0), ('pT', 1), 'ss'])
                act_fence(['ss'])
                A(lambda e: e.activation(out=lnv[:, :], in_=ss[:, :], func=AF.Ln, scale=1.0 / D, bias=epst[:, 0:1]),
                  ['ss', 'fence_out', 'epst'], ['lnv'])
                A(lambda e: e.activation(out=rstd[:, :], in_=lnv[:, :], func=AF.Exp, scale=-0.5), ['lnv'], ['rstd'])
            for j in range(1, NT):
                if do_norm:
                    V(lambda e, j=j: e.scalar_tensor_tensor(out=h[:, j, :], in0=h[:, j, :], scalar=rstd[:, j:j + 1],
                                                            in1=cst[:, C_GFIN:C_GFIN + D], op0=ALU.mult, op1=ALU.mult),
                      [('h', j), 'rstd', 'cst'], [('h', j)])
                S.add('sp', lambda e, j=j: e.dma_start(out=y_d[s, (j - 1) * 128:j * 128, :], in_=h[:, j, :]),
                      [('h', j)], [], dma=True)

        def load_group(s, g):
            for j in TOK_GROUPS[g]:
                if j == 0:
                    S.add('sp', lambda e: e.dma_start(out=h[0:NMETA, 0, :], in_=meta_d[:, :]), [], [('h', 0)], dma=True)
                else:
                    S.add('sp', lambda e, j=j, s=s: e.dma_start(out=h[:, j, :], in_=x_d[s, (j - 1) * 128:j * 128, :]),
                          [], [('h', j)], dma=True)

        def final_group(s, g):
            tl = [j for j in TOK_GROUPS[g] if j >= 1]
            if not tl:
                return
            j0, j1 = tl[0], tl[-1] + 1
            for j in tl:
                A(lambda e, j=j: e.activation(out=junk[:, :], in_=h[:, j, :], func=AF.Square, accum_out=ss[:, j:j + 1]),
                  [('h', j)], [('pT', 0), ('pT', 1), ('ss', g)])
            act_fence([('ss', g)])
            A(lambda e: e.activation(out=lnv[:, j0:j1], in_=ss[:, j0:j1], func=AF.Ln, scale=1.0 / D, bias=epst[:, 0:1]),
              [('ss', g), 'fence_out', 'epst'], [('lnv', g)])
            A(lambda e: e.activation(out=rstd[:, j0:j1], in_=lnv[:, j0:j1], func=AF.Exp, scale=-0.5), [('lnv', g)], [('rstd', g)])
            for j in tl:
                V(lambda e, j=j: e.scalar_tensor_tensor(out=h[:, j, :], in0=h[:, j, :], scalar=rstd[:, j:j + 1],
                                                        in1=cst[:, C_GFIN:C_GFIN + D], op0=ALU.mult, op1=ALU.mult),
                  [('h', j), ('rstd', g), 'cst'], [('h', j)])
                S.add('sp', lambda e, j=j: e.dma_start(out=y_d[s, (j - 1) * 128:j * 128, :], in_=h[:, j, :]),
                      [('h', j)], [], dma=True)

        dbg_mode = stop_after is not None
        for s in range(n_seq):
            if s == 0 or dbg_mode:
                for g in range(len(TOK_GROUPS)):
                    load_group(s, g)
            if dbg_mode:
                norm_to_xT(C_G1)
            if dbg_mode:
                ffn(f1_d)
                if stop_after == 'ffn1':
                    final_store(s, do_norm=False); continue
                norm_to_xT(C_GM)
                mixer()
                if stop_after == 'mix':
                    final_store(s, do_norm=False); continue
                norm_to_xT(C_G2)
                ffn(f2_d)
                final_store(s, do_norm=False); continue
            import os
            if os.environ.get('NO_TAILS'):
                if s > 0:
                    for g in range(len(TOK_GROUPS)):
                        load_group(s, g)
                norm_to_xT(C_G1)
                ffn(f1_d)
                norm_to_xT(C_GM)
                mixer()
                norm_to_xT(C_G2)
                ffn(f2_d)
                for g in range(len(TOK_GROUPS)):
                    final_group(s, g)
                continue
            norm_front(0, C_G1)()
            ffn(f1_d, tail=lambda g: norm_front(g, C_GM), head=lambda g: norm_front(g, C_G1)())
            mixer(tail=lambda g: norm_front(g, C_G2))
            def tail2(g, s=s):
                final_group(s, g)
                if s + 1 < n_seq:
                    load_group(s + 1, g)
                return None
            ffn(f2_d, tail=tail2)

        sem_ctx = {e: es.enter_context(nc.semaphore(f"tl_{e}")) for e in COMPUTE}
        dma_sems = {
            'sp': [es.enter_context(nc.semaphore(f"dsp{i}")) for i in range(24)],
            'pool': [es.enter_context(nc.semaphore(f"dpl{i}")) for i in range(8)],
        }
        streams, finals = S.lower(nc, sem_ctx, dma_sems)
        block = es.enter_context(nc.Block())

        def emit(engname, tail_waits=()):
            def f(e):
                for waits, op, sig in streams.get(engname, []):
                    for (sem, val) in waits:
                        e.wait_ge(sem, val)
                    ins = op.fn(e)
                    if sig is not None:
                        ins.then_inc(sig[0], sig[1])
                for (sem, val) in tail_waits:
                    e.wait_ge(sem, val)
            return f

        block.sync(emit('sp', finals.get('sp', [])))
        block.gpsimd(emit('pool', finals.get('pool', [])))
        block.tensor(emit('pe'))
        block.scalar(emit('act'))
        block.vector(emit('dve'))
    return nc


_PROG = {}


def kernel(**inputs):
    inp = {k: np.asarray(v) for k, v in inputs.items()}
    x = np.ascontiguousarray(inp['x'], dtype=np.float32)
    cst = make_consts(inp)
    f1 = ffn_layout(inp['ffn1_w_gu'].astype(np.float32), inp['ffn1_w_down'].astype(np.float32))
    f2 = ffn_layout(inp['ffn2_w_gu'].astype(np.float32), inp['ffn2_w_down'].astype(np.float32))
    mx = mix_layout(inp['w_in'].astype(np.float32), inp['w_out'].astype(np.float32))
    meta = np.ascontiguousarray(inp['meta_tokens'], dtype=np.float32)
    if 'nc' not in _PROG:
        _PROG['nc'] = build_program(SEQ_PER_CORE)
    nc = _PROG['nc']
    in_maps = []
    for c in range(N_CORES):
        in_maps.append({"x": x[c * SEQ_PER_CORE:(c + 1) * SEQ_PER_CORE], "meta": meta, "cst": cst,
                        "ffn1": f1, "ffn2": f2, "mixw": mx})
    res = run_bass_kernel_spmd(nc, in_maps, core_ids=list(range(N_CORES)))
    out = np.concatenate([np.asarray(r["y"]) for r in res.results], axis=0)
    return out.astype(np.float32)
```

```python
import numpy as np
import concourse.bass as bass
import concourse.mybir as mybir
from concourse.bass_utils import run_bass_kernel_spmd

F32 = mybir.dt.float32
BF16 = mybir.dt.bfloat16
AF = mybir.ActivationFunctionType
ALU = mybir.AluOpType
AX = mybir.AxisListType

D = 1024
SEQ = 2048
NMETA = 16
L = SEQ + NMETA
NT = 17
DFF = 2816
NCH = DFF // 128
EPS = 1e-6
N_CORES = 8
SEQ_PER_CORE = 4
FF_GROUPS = [(0, 4), (4, 4), (8, 4), (12, 4), (16, 4), (20, 2)]
SLOT_ELEMS = 4096
N_SLOTS = 5
MASK_NEG = -30000.0


def tile_rows(j):
    return NMETA if j == 0 else 128


def tile_col0(j):
    return 0 if j == 0 else NMETA + (j - 1) * 128


TOK_GROUPS = [[0], [1, 2, 3, 4], [5, 6, 7, 8], [9, 10, 11, 12], [13, 14, 15, 16]]


def group_cols(g):
    tl = TOK_GROUPS[g]
    c0 = tile_col0(tl[0])
    c1 = tile_col0(tl[-1]) + tile_rows(tl[-1])
    return c0, c1


C_IDENT = 0
C_NEGTRI = 128
C_NEGONES = 256
C_MASK = 384
C_BLK = 512
C_G1 = 640
C_GM = 648
C_G2 = 656
C_GFIN = 664
C_CONVW = C_GFIN + 1024
C_GCONV = C_CONVW + 12
C_GATTN = C_GCONV + 4
C_BF = C_GATTN + 4
C_TOTAL = C_BF + 8


def make_consts(inp):
    c = np.zeros((128, C_TOTAL), np.float32)
    idx = np.arange(128)
    c[:, C_IDENT:C_IDENT + 128] = np.eye(128, dtype=np.float32)
    c[:, C_NEGTRI:C_NEGTRI + 128] = np.where(idx[:, None] <= idx[None, :], -1.0, 0.0)
    c[:, C_NEGONES:C_NEGONES + 128] = -1.0
    c[:, C_MASK:C_MASK + 128] = np.where(idx[:, None] <= idx[None, :], 0.0, MASK_NEG)
    c[:, C_BLK:C_BLK + 128] = np.where((idx[:, None] // 64) == (idx[None, :] // 64), 1.0 / 64, 0.0)
    c[:, C_G1:C_G1 + 8] = inp['ffn1_norm'].reshape(8, 128).T
    c[:, C_GM:C_GM + 8] = inp['mix_norm'].reshape(8, 128).T
    c[:, C_G2:C_G2 + 8] = inp['ffn2_norm'].reshape(8, 128).T
    c[:, C_GFIN:C_GFIN + 1024] = inp['final_norm'].reshape(1, 1024)
    cw = inp['conv_w'].reshape(3, 4, 128)
    c[:, C_CONVW:C_CONVW + 12] = cw.transpose(2, 1, 0).reshape(128, 12)
    c[:, C_GCONV:C_GCONV + 4] = inp['out_norm_conv'].reshape(4, 128).T
    c[:, C_GATTN:C_GATTN + 4] = inp['out_norm_attn'].reshape(4, 128).T
    c[:, C_BF:C_BF + 8] = inp['b_f'].reshape(1, 8)
    return np.ascontiguousarray(c)


def ffn_layout(w_gu, w_down):
    w_gu = w_gu.reshape(D, 2 * DFF)
    w_down = w_down.reshape(DFF, D)
    gu = w_gu.reshape(8, 128, 2 * DFF).transpose(1, 0, 2)
    dn = w_down.reshape(NCH, 128, D).transpose(1, 0, 2)
    pieces = []
    for (c0, n) in FF_GROUPS:
        pieces.append(gu[:, :, c0 * 128:(c0 + n) * 128].reshape(128, -1))
        pieces.append(gu[:, :, DFF + c0 * 128:DFF + (c0 + n) * 128].reshape(128, -1))
        pieces.append(dn[:, c0:c0 + n, :].reshape(128, -1))
    return np.ascontiguousarray(np.concatenate(pieces, axis=1))


def ffn_piece_offsets():
    offs = []
    o = 0
    for (c0, n) in FF_GROUPS:
        g = (o, 8 * n * 128); o += 8 * n * 128
        u = (o, 8 * n * 128); o += 8 * n * 128
        d = (o, n * 1024); o += n * 1024
        offs.append((g, u, d))
    return offs, o


MIX_FG = 64
MIX_UNIT = 8 * 3 * 128 + 1024


def mix_layout(w_in, w_out):
    w_in = w_in.reshape(D, 3080)
    w_out = w_out.reshape(D, D)
    wi = w_in.reshape(8, 128, 3080).transpose(1, 0, 2)
    wo = w_out.reshape(8, 128, D).transpose(1, 0, 2)
    pieces = [wi[:, :, 3072:3080].reshape(128, -1)]
    for cc in range(4):
        blk = np.stack([wi[:, :, cc * 128:(cc + 1) * 128],
                        wi[:, :, 512 + cc * 128:512 + (cc + 1) * 128],
                        wi[:, :, 1024 + cc * 128:1024 + (cc + 1) * 128]], axis=2)
        pieces.append(blk.reshape(128, -1))
        pieces.append(wo[:, cc, :])
    for p in range(4):
        blk = np.stack([wi[:, :, 1536 + p * 128:1536 + (p + 1) * 128],
                        wi[:, :, 2048 + p * 128:2048 + (p + 1) * 128],
                        wi[:, :, 2560 + p * 128:2560 + (p + 1) * 128]], axis=2)
        pieces.append(blk.reshape(128, -1))
        pieces.append(wo[:, 4 + p, :])
    return np.ascontiguousarray(np.concatenate(pieces, axis=1))


MIX_TOTAL = MIX_FG + 8 * MIX_UNIT


COMPUTE = ('pe', 'act', 'dve')


class Op:
    __slots__ = ('eng', 'fn', 'deps', 'dma', 'idx', 'sig', 'sigval', 'sem', 'prewait')

    def __init__(self, eng, fn, dma, idx):
        self.eng = eng; self.fn = fn; self.dma = dma; self.idx = idx
        self.deps = set(); self.sig = False; self.sigval = 0; self.sem = None; self.prewait = None


class Sched:
    def __init__(self):
        self.ops = []
        self.lastw = {}
        self.readers = {}

    def add(self, eng, fn, reads=(), writes=(), dma=False):
        idx = len(self.ops)
        op = Op(eng, fn, dma, idx)
        ops = self.ops
        wset = set(writes)
        for k in reads:
            w = self.lastw.get(k)
            if w is not None:
                op.deps.add(w)
        for k in wset:
            w = self.lastw.get(k)
            if w is not None:
                op.deps.add(w)
            for r in self.readers.get(k, {}).values():
                for ri in (r if isinstance(r, list) else [r]):
                    ro = ops[ri]
                    if ro.eng == eng and not ro.dma and not dma:
                        continue
                    op.deps.add(ri)
        for k in wset:
            self.lastw[k] = idx
            self.readers[k] = {}
        for k in reads:
            if k in wset:
                continue
            rd = self.readers.setdefault(k, {})
            if dma:
                rd.setdefault(('dma', eng), []).append(idx)
            else:
                rd[eng] = idx
        if eng == 'pe':
            op.deps = {d for d in op.deps if not (ops[d].eng == 'pe')}
        ops.append(op)
        return idx

    def lower(self, nc, sems, dma_sems):
        ops = self.ops
        for op in ops:
            for d in op.deps:
                ops[d].sig = True
        cnt = {e: 0 for e in sems}
        dcount = {}
        drr = {q: 0 for q in dma_sems}
        for op in ops:
            if op.dma:
                pool = dma_sems[op.eng]
                m = drr[op.eng] % len(pool)
                drr[op.eng] += 1
                sem = pool[m]
                prev = dcount.get((op.eng, m), 0)
                op.prewait = (sem, 16 * prev) if prev > 0 else None
                dcount[(op.eng, m)] = prev + 1
                op.sem = sem
                op.sigval = 16 * (prev + 1)
            elif op.sig:
                cnt[op.eng] += 1
                op.sem = sems[op.eng]
                op.sigval = cnt[op.eng]
        streams = {}
        waited = {}
        for op in ops:
            st = streams.setdefault(op.eng, [])
            need = {}
            if op.prewait is not None:
                need[id(op.prewait[0])] = op.prewait
            for d in op.deps:
                do = ops[d]
                key = id(do.sem)
                if key not in need or need[key][1] < do.sigval:
                    need[key] = (do.sem, do.sigval)
            waits = []
            for key, (sem, val) in need.items():
                wk = (op.eng, key)
                if waited.get(wk, 0) >= val:
                    continue
                waited[wk] = val
                waits.append((sem, val))
            sig = None
            if op.dma:
                sig = (op.sem, 16)
            elif op.sig:
                sig = (op.sem, 1)
            st.append((waits, op, sig))
        finals = {}
        for (q, m), c in dcount.items():
            finals.setdefault(q, []).append((dma_sems[q][m], 16 * c))
        return streams, finals


class Rot:
    def __init__(self, n):
        self.n = n; self.i = -1

    def next(self):
        self.i = (self.i + 1) % self.n
        return self.i


def build_program(n_seq=SEQ_PER_CORE, stop_after=None):
    nc = bass.Bass("TRN2", target_bir_lowering=False)
    ffn_offs, ffn_total = ffn_piece_offsets()
    x_d = nc.dram_tensor("x", [n_seq, SEQ, D], F32, kind="ExternalInput").ap()
    meta_d = nc.dram_tensor("meta", [NMETA, D], F32, kind="ExternalInput").ap()
    cst_d = nc.dram_tensor("cst", [128, C_TOTAL], F32, kind="ExternalInput").ap()
    f1_d = nc.dram_tensor("ffn1", [128, ffn_total], F32, kind="ExternalInput").ap()
    f2_d = nc.dram_tensor("ffn2", [128, ffn_total], F32, kind="ExternalInput").ap()
    mx_d = nc.dram_tensor("mixw", [128, MIX_TOTAL], F32, kind="ExternalInput").ap()
    y_d = nc.dram_tensor("y", [n_seq, SEQ, D], F32, kind="ExternalOutput").ap()
    if stop_after == 'mix':
        dbg_d = nc.dram_tensor("dbg", [128, 1024], F32, kind="ExternalOutput").ap()
        dbg2_d = nc.dram_tensor("dbg2", [128, NT * 128 + L], F32, kind="ExternalOutput").ap()

    S = Sched()
    from contextlib import ExitStack
    es = ExitStack()

    def sb(name, shape, dt):
        return es.enter_context(nc.sbuf_tensor(name, shape, dt))

    with es:
        h = sb("h", [128, NT, D], F32)
        xT = sb("xT", [128, 8, L], BF16)
        cst = sb("cst_sb", [128, C_TOTAL], F32)
        ident = sb("ident", [128, 128], BF16)
        maskb = sb("maskb", [128, 128], BF16)
        zeros = sb("zeros", [128, 264], BF16)
        ring = sb("ring", [128, N_SLOTS, SLOT_ELEMS], BF16)
        ss = sb("ss", [128, NT], F32)
        lnv = sb("lnv", [128, NT], F32)
        rstd = sb("rstd", [128, NT], F32)
        fence = sb("fence", [128, 8], F32)
        xs = sb("xs", [128, 2, D], BF16)
        sg = sb("sg", [128, 2, 512], F32)
        actT = sb("actT", [128, 2, 4, 512], BF16)
        zt = sb("zt", [128, NT, 8], F32)
        Cc = sb("Cc", [128, NT + 1, 8], F32)
        Ft = sb("Ft", [128, NT, 8], F32)
        NPAIR = 9
        Rp = sb("Rp", [128, NPAIR, 8], F32)
        biast = sb("biast", [128, 2, NPAIR, NT], F32)
        qT = sb("qT", [128, L], BF16)
        kA = sb("kA", [128, L], BF16)
        kB = sb("kB", [128, L], BF16)
        vaug = sb("vaug", [128, NT, 2, 65], BF16)
        ub = sb("ub", [128, 514], F32)
        cS = sb("cS", [128, 512], F32)
        acc = sb("acc", [128, 512], F32)
        yc = sb("yc", [128, 512], F32)
        yT = sb("yT", [128, 2, L], BF16)
        pT = sb("pT", [128, 3, 512], BF16)
        junk = pT[:, 0:2, :].rearrange("p a b -> p (a b)")
        lsp = zt
        sqo = cS[:, 0:256].rearrange("p (a b) -> p a b", b=64)
        yv = cS[:, 256:512].rearrange("p (a b) -> p a b", b=64)
        ysq = cS
        lnr = acc
        rr = acc
        ytok = sb("ytok", [128, NT, 128], BF16)
        sso = sb("sso", [128, 4], F32)
        zc = sb("zc", [128, 4], F32)
        z2 = sb("z2", [128, 4], F32)
        lnt = sb("lnt", [128, 4], F32)
        ro = sb("ro", [128, 4], F32)

        psA = [es.enter_context(nc.psum_tensor(f"psA{i}", [128, 512], F32)) for i in range(4)]
        psB = [es.enter_context(nc.psum_tensor(f"psB{i}", [128, 1024], F32)) for i in range(2)]

        def bank(b):
            if b < 4:
                return psA[b][:, :]
            return psB[(b - 4) // 2][:, ((b - 4) % 2) * 512:((b - 4) % 2) * 512 + 512]

        rotA = Rot(4)
        rotB = Rot(2)
        rotO = Rot(2)
        rot_xs = Rot(2); rot_sg = Rot(2); rot_act = Rot(2); rot_pT = Rot(3)
        ring_rot = Rot(N_SLOTS)

        def cc(col, n=1):
            return cst[:, col:col + n]

        def A(fn, reads, writes):
            S.add('act', fn, reads, writes)

        def V(fn, reads, writes):
            S.add('dve', fn, reads, writes)

        def P(fn, reads, writes):
            S.add('pe', fn, reads, writes)

        def act_fence(reads):
            A(lambda e: e.activation(out=fence[:, 0:1], in_=fence[:, 1:2], func=AF.Copy),
              list(reads) + ['fence_in'], ['fence_out'])

        S.add('sp', lambda e: e.dma_start(out=cst[:, :], in_=cst_d[:, :]), [], ['cst'], dma=True)
        V(lambda e: e.tensor_copy(out=ident[:, :], in_=cst[:, C_IDENT:C_IDENT + 128]), ['cst'], ['ident'])
        V(lambda e: e.tensor_copy(out=maskb[:, :], in_=cst[:, C_MASK:C_MASK + 128]), ['cst'], ['maskb'])
        V(lambda e: e.memset(zeros[:, :], 0.0), [], ['zeros'])
        V(lambda e: e.memset(fence[:, :], 0.0), [], ['fence_in', 'fence_out'])
        V(lambda e: e.memset(ss[:, :], 1.0), [], [('ss', g_) for g_ in range(5)])
        V(lambda e: e.memset(zt[:, :, :], 0.0), [], ['zt'])
        V(lambda e: e.memset(Cc[:, :, :], 0.0), [], ['Cc'])
        V(lambda e: e.memset(ub[:, :], 0.0), [], ['ub0'])
        V(lambda e: e.memset(vaug[:, :, :, :], 1.0), [], [('vaug', j_) for j_ in range(NT)])
        V(lambda e: e.memset(h[:, 0, :], 0.0), [], [('h', 0)])
        V(lambda e: e.memset(ytok[:, :, :], 0.0), [], ['ytok'])
        V(lambda e: e.memset(Ft[:, :, :], 0.0), [], ['Ft'])
        V(lambda e: e.memset(lnv[:, :], 0.0), [], [('lnv', g_) for g_ in range(5)])
        V(lambda e: e.memset(rstd[:, :], 1.0), [], [('rstd', g_) for g_ in range(5)])
        V(lambda e: e.memset(kA[:, :], 0.0), [], [('kT', g_) for g_ in range(5)])
        V(lambda e: e.memset(kB[:, :], 0.0), [], [('kT', g_) for g_ in range(5)])

        def load_piece(src, off, n):
            slot = ring_rot.next()
            assert n <= SLOT_ELEMS
            S.add('pool', lambda e: e.dma_start(out=ring[:, slot, 0:n], in_=src[:, off:off + n],
                                                max_dma_last_dim=8192),
                  [], [('ring', slot)], dma=True)
            return slot

        def norm_front(g, gcol):
            tl = TOK_GROUPS[g]
            j0, j1 = tl[0], tl[-1] + 1
            for j in tl:
                R = tile_rows(j)
                A(lambda e, j=j, R=R: e.activation(out=junk[:R, :], in_=h[:R, j, :], func=AF.Square,
                                                   accum_out=ss[:R, j:j + 1]),
                  [('h', j)], [('pT', 0), ('pT', 1), ('ss', g)])
            act_fence([('ss', g)])
            A(lambda e: e.activation(out=lnv[:, j0:j1], in_=ss[:, j0:j1], func=AF.Ln, scale=1.0 / D, bias=epst[:, 0:1]),
              [('ss', g), 'fence_out', 'epst'], [('lnv', g)])
            A(lambda e: e.activation(out=rstd[:, j0:j1], in_=lnv[:, j0:j1], func=AF.Exp, scale=-0.5), [('lnv', g)], [('rstd', g)])

            def back():
                for j in tl:
                    R = tile_rows(j)
                    c0 = tile_col0(j)
                    xb = rot_xs.next()
                    if j % 2 == 0:
                        A(lambda e, j=j, R=R, xb=xb: e.activation(out=xs[:R, xb, :], in_=h[:R, j, :], func=AF.Copy,
                                                                  scale=rstd[:R, j:j + 1]),
                          [('h', j), ('rstd', g)], [('xs', xb)])
                    else:
                        V(lambda e, j=j, R=R, xb=xb: e.tensor_scalar(out=xs[:R, xb, :], in0=h[:R, j, :], scalar1=rstd[:R, j:j + 1],
                                                                     scalar2=None, op0=ALU.mult),
                          [('h', j), ('rstd', g)], [('xs', xb)])
                    b = rotA.next()
                    tp = bank(b).bitcast(BF16)
                    tp3 = tp.rearrange('p (k t) -> p k t', t=128)
                    for k in range(8):
                        P(lambda e, k=k, R=R, xb=xb, tp3=tp3: e.transpose(out=tp3[:, k, :R],
                                                                        in_=xs[:R, xb, k * 128:(k + 1) * 128],
                                                                        identity=ident[:R, :R]),
                          [('xs', xb), 'ident'], [('ps', b)])
                    gbc = cst[:, gcol:gcol + 8].unsqueeze(2).to_broadcast([128, 8, R])
                    V(lambda e, R=R, c0=c0, tp3=tp3, gbc=gbc: e.tensor_tensor(out=xT[:, :, c0:c0 + R], in0=tp3[:, :, :R],
                                                                              in1=gbc, op=ALU.mult),
                      [('ps', b), 'cst'], [('xT', j)])
            return back

        def norm_to_xT(gcol):
            for g in range(len(TOK_GROUPS)):
                norm_front(g, gcol)()

        epst = sb("epst", [128, 1], F32)
        V(lambda e: e.memset(epst[:, :], EPS), [], ['epst'])
        onet = sb("onet", [128, 1], F32)
        V(lambda e: e.memset(onet[:, :], 1.0), [], ['onet'])

        def ffn(src_d, tail=None, head=None):
            for gi, (c0, n) in enumerate(FF_GROUPS):
                last_grp = (gi == len(FF_GROUPS) - 1)
                deferred = None
                (go, gl), (uo, ul), (do_, dl) = ffn_offs[gi]
                sg_slot = load_piece(src_d, go, gl)
                su_slot = load_piece(src_d, uo, ul)
                sd_slot = load_piece(src_d, do_, dl)
                wg = ring[:, sg_slot, 0:gl].rearrange('p (k c) -> p k c', k=8)
                wu = ring[:, su_slot, 0:ul].rearrange('p (k c) -> p k c', k=8)
                wd = ring[:, sd_slot, 0:dl].rearrange('p (i c) -> p i c', c=D)
                for g in range(len(TOK_GROUPS)):
                    t0, t1_ = group_cols(g)
                    T = t1_ - t0
                    xkeys = [('xT', j) for j in TOK_GROUPS[g]]
                    ab = rot_act.next()
                    for i in range(n):
                        bg = rotA.next(); bu = rotA.next()
                        for k in range(8):
                            P(lambda e, i=i, k=k, bg=bg, wg=wg, t0=t0, T=T: e.matmul(
                                out=bank(bg)[:, :T], lhsT=wg[:, k, i * 128:(i + 1) * 128], rhs=xT[:, k, t0:t0 + T],
                                start=(k == 0), stop=(k == 7)),
                              xkeys + [('ring', sg_slot)], [('ps', bg)])
                        for k in range(8):
                            P(lambda e, i=i, k=k, bu=bu, wu=wu, t0=t0, T=T: e.matmul(
                                out=bank(bu)[:, :T], lhsT=wu[:, k, i * 128:(i + 1) * 128], rhs=xT[:, k, t0:t0 + T],
                                start=(k == 0), stop=(k == 7)),
                              xkeys + [('ring', su_slot)], [('ps', bu)])
                        sb_ = rot_sg.next()
                        A(lambda e, bg=bg, sb_=sb_, T=T: e.activation(out=sg[:, sb_, :T], in_=bank(bg)[:, :T], func=AF.Silu),
                          [('ps', bg)], [('sg', sb_)])
                        V(lambda e, bu=bu, sb_=sb_, ab=ab, i=i, T=T: e.tensor_tensor(
                            out=actT[:, ab, i, :T], in0=bank(bu)[:, :T], in1=sg[:, sb_, :T], op=ALU.mult),
                          [('ps', bu), ('sg', sb_)], [('actT', ab, i)])
                    if deferred is not None:
                        deferred(); deferred = None
                    if gi == 0 and head is not None and g + 1 < len(TOK_GROUPS):
                        head(g + 1)
                    for j in TOK_GROUPS[g]:
                        R = tile_rows(j)
                        lc = tile_col0(j) - t0
                        pb = rotB.next()
                        dps = psB[pb]
                        for half in range(2):
                            for i in range(n):
                                P(lambda e, i=i, half=half, dps=dps, ab=ab, lc=lc, R=R, wd=wd, n=n: e.matmul(
                                    out=dps[:R, half * 512:(half + 1) * 512], lhsT=actT[:, ab, i, lc:lc + R],
                                    rhs=wd[:, i, half * 512:(half + 1) * 512], start=(i == 0), stop=(i == n - 1)),
                                  [('actT', ab, i), ('ring', sd_slot)], [('ps', 4 + 2 * pb + half)])
                        V(lambda e, dps=dps, R=R, j=j: e.scalar_tensor_tensor(
                            out=h[:R, j, :], in0=dps[:R, :], scalar=0.5, in1=h[:R, j, :], op0=ALU.mult, op1=ALU.add),
                          [('ps', 4 + 2 * pb), ('ps', 5 + 2 * pb), ('h', j)], [('h', j)])
                    if last_grp and tail is not None:
                        deferred = tail(g)
                if deferred is not None:
                    deferred(); deferred = None

        def wout_partial(wo_aps, wo_keys, tail=None):
            deferred = None
            for j in range(NT):
                R = tile_rows(j)
                c0 = tile_col0(j)
                pb = rotB.next()
                dps = psB[pb]
                for half in range(2):
                    for s_ in range(2):
                        P(lambda e, s_=s_, half=half, dps=dps, c0=c0, R=R: e.matmul(
                            out=dps[:R, half * 512:(half + 1) * 512], lhsT=yT[:, s_, c0:c0 + R],
                            rhs=wo_aps[s_][:, half * 512:(half + 1) * 512], start=(s_ == 0), stop=(s_ == 1)),
                          [('yT', s_), wo_keys[s_]], [('ps', 4 + 2 * pb + half)])
                V(lambda e, dps=dps, R=R, j=j: e.tensor_tensor(out=h[:R, j, :], in0=dps[:R, :], in1=h[:R, j, :], op=ALU.add),
                  [('ps', 4 + 2 * pb), ('ps', 5 + 2 * pb), ('h', j)], [('h', j)])
                gdone = [g_ for g_, tl_ in enumerate(TOK_GROUPS) if tl_[-1] == j]
                if gdone:
                    if deferred is not None:
                        deferred(); deferred = None
                    if tail is not None:
                        deferred = tail(gdone[0])
            if deferred is not None:
                deferred()

        def wout_items(wo_aps, wo_keys):
            items = []
            for j in range(NT):
                def it(j=j):
                    R = tile_rows(j); c0 = tile_col0(j)
                    for half in range(2):
                        b = rotA.next()
                        for s_ in range(2):
                            P(lambda e, s_=s_, half=half, b=b: e.matmul(
                                out=bank(b)[:R, :], lhsT=yT[:, s_, c0:c0 + R],
                                rhs=wo_aps[s_][:, half * 512:(half + 1) * 512], start=(s_ == 0), stop=(s_ == 1)),
                              [('yT', s_), wo_keys[s_]], [('ps', b)])
                        V(lambda e, half=half, b=b: e.tensor_tensor(out=h[:R, j, half * 512:(half + 1) * 512], in0=bank(b)[:R, :],
                                                                    in1=h[:R, j, half * 512:(half + 1) * 512], op=ALU.add),
                          [('ps', b), ('h', j)], [('h', j)])
                items.append(it)
            return items

        def mixer(tail=None):
            sfg = load_piece(mx_d, 0, MIX_FG)
            wfg = ring[:, sfg, 0:MIX_FG].rearrange('p (k c) -> p k c', k=8)
            for j in range(NT):
                R = tile_rows(j); c0 = tile_col0(j)
                b = rotA.next()
                for k in range(8):
                    P(lambda e, k=k, b=b, R=R, c0=c0: e.matmul(out=bank(b)[:R, 0:8], lhsT=xT[:, k, c0:c0 + R],
                                                              rhs=wfg[:, k, :], start=(k == 0), stop=(k == 7)),
                      [('xT', j), ('ring', sfg)], [('ps', b)])
                V(lambda e, b=b, R=R, j=j: e.tensor_tensor(out=zt[:R, j, :], in0=bank(b)[:R, 0:8],
                                                           in1=cst[:R, C_BF:C_BF + 8], op=ALU.add),
                  [('ps', b), 'cst'], ['zt'])
            A(lambda e: e.activation(out=zt[:, :, :], in_=zt[:, :, :], func=AF.Exp, scale=-1.0), ['zt'], ['zt'])
            A(lambda e: e.activation(out=zt[:, :, :], in_=zt[:, :, :], func=AF.Ln, bias=onet[:, 0:1]), ['zt', 'onet'], ['zt', 'lsp'])
            for j in range(NT):
                R = tile_rows(j)
                b1 = rotA.next(); b2 = rotA.next()
                P(lambda e, b1=b1, R=R, j=j: e.matmul(out=bank(b1)[:R, 0:8], lhsT=cst[:R, C_NEGTRI:C_NEGTRI + R],
                                                      rhs=lsp[:R, j, :], start=True, stop=True),
                  ['lsp', 'cst'], [('ps', b1)])
                P(lambda e, b2=b2, R=R, j=j: e.matmul(out=bank(b2)[:, 0:8], lhsT=cst[:R, C_NEGONES:C_NEGONES + 128],
                                                      rhs=lsp[:R, j, :], start=True, stop=True),
                  ['lsp', 'cst'], [('ps', b2)])
                V(lambda e, b1=b1, R=R, j=j: e.tensor_tensor(out=Ft[:R, j, :], in0=bank(b1)[:R, 0:8], in1=Cc[:R, j, :], op=ALU.add),
                  [('ps', b1), 'Cc'], ['Ft'])
                V(lambda e, b2=b2, j=j: e.tensor_tensor(out=Cc[:, j + 1, :], in0=bank(b2)[:, 0:8], in1=Cc[:, j, :], op=ALU.add),
                  [('ps', b2), 'Cc'], ['Cc'])
            V(lambda e: e.tensor_copy(out=Rp[:, 0, :], in_=Cc[:, 1, :]), ['Cc'], ['Rp'])
            V(lambda e: e.tensor_copy(out=Rp[:, 1:NPAIR, :], in_=Cc[:, 2:NT + 1, :].rearrange('p (a two) h -> p a two h', two=2)[:, :, 0, :]),
              ['Cc'], ['Rp'])
            pending_fin = []

            ALIAS_KEYS = [('actT', a_, i_) for a_ in range(2) for i_ in range(4)] + [('sg', 0), ('sg', 1)]
            SET_KEYS = [['ub0', 'cS', 'acc', 'yc'], ['ub1', 'cS1', 'acc1', 'yc1']]
            V(lambda e: e.memset(fence[:, 2:3], 0.0), [], ALIAS_KEYS + SET_KEYS[1])
            actF = actT[:, :, :, :].rearrange('p a b c -> p (a b c)').bitcast(F32)
            sgF = sg[:, :, :].rearrange('p a b -> p (a b)')
            ubs = [ub, actF[:, 0:514]]
            cSs = [cS, actF[:, 514:1026]]
            accs = [acc, actF[:, 1026:1538]]
            ycs = [yc, sgF[:, 0:512]]
            it = 0
            pend_back = None
            for pair in range(2):
                wo_aps = []; wo_keys = []
                for s_ in range(2):
                    cch = pair * 2 + s_
                    off = MIX_FG + cch * MIX_UNIT
                    sw = load_piece(mx_d, off, MIX_UNIT)
                    wcv = ring[:, sw, 0:3072].rearrange('p (k t c) -> p k t c', k=8, t=3)
                    wo_aps.append(ring[:, sw, 3072:4096]); wo_keys.append(('ring', sw))
                    for g in range(len(TOK_GROUPS)):
                        t0, t1_ = group_cols(g); T = t1_ - t0
                        xkeys = [('xT', j) for j in TOK_GROUPS[g]]
                        p = it % 2; it += 1
                        kub, kcS, kacc, kyc = SET_KEYS[p]
                        ub_, cS_, acc_, yc_ = ubs[p], cSs[p], accs[p], ycs[p]
                        ubprev, kubprev = ubs[1 - p], SET_KEYS[1 - p][0]
                        bc_ = rotA.next(); bh_ = rotA.next(); bb_ = 4 + p; bst = 6 + p
                        for which, bk in ((1, bc_), (2, bh_), (0, bb_)):
                            for k in range(8):
                                P(lambda e, k=k, bk=bk, which=which, wcv=wcv, t0=t0, T=T: e.matmul(
                                    out=bank(bk)[:, :T], lhsT=wcv[:, k, which, :], rhs=xT[:, k, t0:t0 + T],
                                    start=(k == 0), stop=(k == 7)),
                                  xkeys + [('ring', sw)], [('ps', bk)])
                        A(lambda e, bc_=bc_, T=T, cS_=cS_: e.activation(out=cS_[:, :T], in_=bank(bc_)[:, :T], func=AF.Copy),
                          [('ps', bc_)], [kcS])
                        if g == 0:
                            V(lambda e, ub_=ub_: e.memset(ub_[:, 0:2], 0.0), [], [kub])
                        else:
                            V(lambda e, Tp=Tprev, ub_=ub_, ubprev=ubprev: e.tensor_copy(out=ub_[:, 0:2], in_=ubprev[:, Tp:Tp + 2]),
                              [kubprev], [kub])
                        Tprev = T
                        V(lambda e, bh_=bh_, T=T, ub_=ub_, cS_=cS_: e.tensor_tensor(out=ub_[:, 2:2 + T], in0=bank(bh_)[:, :T],
                                                                                  in1=cS_[:, :T], op=ALU.mult),
                          [('ps', bh_), kcS], [kub])
                        w0 = cst[:, C_CONVW + cch * 3 + 0:C_CONVW + cch * 3 + 1]
                        w1 = cst[:, C_CONVW + cch * 3 + 1:C_CONVW + cch * 3 + 2]
                        w2 = cst[:, C_CONVW + cch * 3 + 2:C_CONVW + cch * 3 + 3]
                        V(lambda e, T=T, w2=w2, ub_=ub_, acc_=acc_: e.tensor_scalar(out=acc_[:, :T], in0=ub_[:, 2:2 + T], scalar1=w2,
                                                                                 scalar2=None, op0=ALU.mult), [kub, 'cst'], [kacc])
                        V(lambda e, T=T, w1=w1, ub_=ub_, acc_=acc_: e.scalar_tensor_tensor(out=acc_[:, :T], in0=ub_[:, 1:1 + T], scalar=w1,
                                                                                        in1=acc_[:, :T], op0=ALU.mult, op1=ALU.add),
                          [kub, 'cst', kacc], [kacc])
                        V(lambda e, T=T, w0=w0, ub_=ub_, acc_=acc_: e.scalar_tensor_tensor(out=acc_[:, :T], in0=ub_[:, 0:T], scalar=w0,
                                                                                        in1=acc_[:, :T], op0=ALU.mult, op1=ALU.add),
                          [kub, 'cst', kacc], [kacc])
                        V(lambda e, bb_=bb_, T=T, yc_=yc_, acc_=acc_: e.tensor_tensor(out=yc_[:, :T], in0=bank(bb_)[:, :T], in1=acc_[:, :T], op=ALU.mult),
                          [('ps', bb_), kacc], [kyc])
                        A(lambda e, T=T, yc_=yc_, cS_=cS_: e.activation(out=cS_[:, :T], in_=yc_[:, :T], func=AF.Square), [kyc], [kcS])

                        def back(T=T, t0=t0, s_=s_, cch=cch, bst=bst, cS_=cS_, acc_=acc_, yc_=yc_, kcS=kcS, kacc=kacc, kyc=kyc):
                            P(lambda e: e.matmul(out=bank(bst)[:, :T], lhsT=cst[:, C_BLK:C_BLK + 128], rhs=cS_[:, :T],
                                                 start=True, stop=True), [kcS, 'cst'], [('ps', bst)])
                            A(lambda e: e.activation(out=acc_[:, :T], in_=bank(bst)[:, :T], func=AF.Ln, bias=epst[:, 0:1]),
                              [('ps', bst), 'epst'], [kacc])
                            A(lambda e: e.activation(out=acc_[:, :T], in_=acc_[:, :T], func=AF.Exp, scale=-0.5), [kacc], [kacc])
                            gcv = cst[:, C_GCONV + cch:C_GCONV + cch + 1]
                            V(lambda e: e.scalar_tensor_tensor(
                                out=yT[:, s_, t0:t0 + T], in0=yc_[:, :T], scalar=gcv, in1=acc_[:, :T], op0=ALU.mult, op1=ALU.mult),
                              [kyc, kacc, 'cst'], [('yT', s_)])
                        if pend_back is not None:
                            pend_back()
                        pend_back = back
                if pend_back is not None:
                    pend_back(); pend_back = None
                if pair == 0:
                    wout_partial(wo_aps, wo_keys)
                else:
                    bg_work = wout_items(wo_aps, wo_keys)
            V(lambda e: e.memset(fence[:, 3:4], 0.0), [], ALIAS_KEYS + SET_KEYS[1])

            for pair in range(2):
                wo_aps = []; wo_keys = []
                for s_ in range(2):
                    hp = pair * 2 + s_
                    off = MIX_FG + (4 + hp) * MIX_UNIT
                    sw = load_piece(mx_d, off, MIX_UNIT)
                    wqkv = ring[:, sw, 0:3072].rearrange('p (k t c) -> p k t c', k=8, t=3)
                    wo_aps.append(ring[:, sw, 3072:4096]); wo_keys.append(('ring', sw))
                    def proj_items(g, wqkv=wqkv, sw=sw):
                        items = []
                        t0, t1_ = group_cols(g); T = t1_ - t0
                        xkeys = [('xT', j) for j in TOK_GROUPS[g]]
                        for which, bkk in ((0, 6), (1, 7)):
                            for k in range(8):
                                def it(k=k, bkk=bkk, which=which):
                                    P(lambda e: e.matmul(
                                        out=bank(bkk)[:, :T], lhsT=wqkv[:, k, which, :], rhs=xT[:, k, t0:t0 + T],
                                        start=(k == 0), stop=(k == 7)),
                                      xkeys + [('ring', sw)], [('ps', bkk)])
                                    if k == 7 and which == 0:
                                        V(lambda e: e.tensor_scalar(out=qT[:, t0:t0 + T], in0=bank(6)[:, :T], scalar1=0.125,
                                                                    scalar2=None, op0=ALU.mult),
                                          [('ps', 6)], [('qT', g)])
                                    if k == 7 and which == 1:
                                        V(lambda e: e.tensor_copy(out=kA[0:64, t0:t0 + T], in_=bank(7)[0:64, :T]), [('ps', 7)], [('kT', g)])
                                        V(lambda e: e.tensor_copy(out=kB[64:128, t0:t0 + T], in_=bank(7)[64:128, :T]), [('ps', 7)], [('kT', g)])
                                items.append(it)
                        for n_, j in enumerate(TOK_GROUPS[g]):
                            R = tile_rows(j); c0 = tile_col0(j)
                            bv = 6 + (n_ % 2)
                            for k in range(8):
                                def it(k=k, bv=bv, R=R, c0=c0, j=j):
                                    P(lambda e: e.matmul(out=bank(bv)[:R, 0:128], lhsT=xT[:, k, c0:c0 + R],
                                                         rhs=wqkv[:, k, 2, :], start=(k == 0), stop=(k == 7)),
                                      [('xT', j), ('ring', sw)], [('ps', bv)])
                                    if k == 7:
                                        V(lambda e: e.tensor_copy(out=vaug[:R, j, :, 0:64],
                                                                  in_=bank(bv)[:R, 0:128].rearrange('p (a d) -> p a d', a=2)),
                                          [('ps', bv)], [('vaug', j)])
                                items.append(it)
                        return items

                    for hh in range(2):
                        hd = hp * 2 + hh
                        rm = Rp[:, :, hd].unsqueeze(2).to_broadcast([128, NPAIR, NT])
                        fk = Ft[:, :, hd].unsqueeze(1).to_broadcast([128, NPAIR, NT])
                        V(lambda e, hh=hh, rm=rm, fk=fk: e.tensor_tensor(out=biast[:, hh, :, :], in0=rm, in1=fk, op=ALU.subtract),
                          ['Rp', 'Ft'], [('biast', hh)])
                    for it_ in proj_items(0):
                        it_()
                    pend_pv = None
                    for g in range(len(TOK_GROUPS)):
                        work = proj_items(g + 1) if g + 1 < len(TOK_GROUPS) else []
                        n_iter = 2 * (TOK_GROUPS[g][-1] + 1)
                        per_it = -(-len(work) // max(1, n_iter - 2))
                        for hh in range(2):
                            blist = TOK_GROUPS[g]
                            nb = len(blist)
                            ob = 4 + rotO.next()
                            op3 = bank(ob)[:, 0:nb * 65].rearrange('p (b d) -> p b d', d=65)
                            P(lambda e, ob=ob, nb=nb: e.matmul(out=bank(ob)[:, 0:nb * 65], lhsT=zeros[0:1, 0:128], rhs=zeros[0:1, 0:nb * 65],
                                                               start=True, stop=True),
                              ['zeros'], [('ps', ob)])

                            def emit_st(j, blist=blist, g=g, hh=hh):
                                Rk = tile_rows(j); kc0 = tile_col0(j)
                                gj = [g_ for g_, tl_ in enumerate(TOK_GROUPS) if j in tl_][0]
                                bl = [b for b in blist if b >= j]
                                q0 = tile_col0(bl[0]); q1 = tile_col0(bl[-1]) + tile_rows(bl[-1])
                                Tq = q1 - q0
                                bs_ = rotA.next()
                                diag = (j in blist)
                                P(lambda e: e.matmul(
                                    out=bank(bs_)[:Rk, :Tq], lhsT=(kA if hh == 0 else kB)[:, kc0:kc0 + Rk], rhs=qT[:, q0:q0 + Tq],
                                    start=True, stop=(not diag)),
                                  [('kT', gj), ('qT', g)], [('ps', bs_)])
                                if diag:
                                    P(lambda e: e.matmul(out=bank(bs_)[:Rk, :Rk], lhsT=ident[:Rk, :Rk], rhs=maskb[:Rk, :Rk],
                                                         start=False, stop=True),
                                      ['ident', 'maskb'], [('ps', bs_)])
                                pb_ = rot_pT.next()
                                pieces = {}
                                for b in bl:
                                    pi = 0 if g == 0 else 2 * (g - 1) + 1 + (1 if (hp * 2 + hh) >= 2 else blist.index(b) // 2)
                                    pieces.setdefault(pi, []).append(b)
                                for pi, bs in pieces.items():
                                    lc0 = tile_col0(bs[0]) - q0
                                    lc1 = tile_col0(bs[-1]) + tile_rows(bs[-1]) - q0
                                    A(lambda e, lc0=lc0, lc1=lc1, pi=pi: e.activation(
                                        out=pT[:Rk, pb_, lc0:lc1], in_=bank(bs_)[:Rk, lc0:lc1], func=AF.Exp,
                                        bias=biast[:Rk, hh, pi, j:j + 1]),
                                      [('ps', bs_), ('biast', hh)], [('pT', pb_)])
                                return (j, Rk, bl, q0, pb_)

                            def emit_pv(st, blist=blist, op3=op3, ob=ob, hh=hh):
                                j, Rk, bl, q0, pb_ = st
                                for b in bl:
                                    Rb = tile_rows(b); lc = tile_col0(b) - q0; bi = blist.index(b)
                                    P(lambda e, Rb=Rb, lc=lc, bi=bi, b=b: e.matmul(
                                        out=op3[:Rb, bi, :], lhsT=pT[:Rk, pb_, lc:lc + Rb], rhs=vaug[:Rk, j, hh, :],
                                        start=False, stop=(j == b), skip_group_check=True),
                                      [('pT', pb_), ('vaug', j)], [('ps', ob)])

                            for j in range(blist[-1] + 1):
                                cur = emit_st(j)
                                if pend_pv is not None:
                                    pend_pv()
                                pend_pv = (lambda cur=cur, emit_pv=emit_pv: emit_pv(cur))
                                if pending_fin and (j == 1 or j == blist[-1]):
                                    pending_fin.pop()()
                                for _ in range(per_it):
                                    if work:
                                        work.pop(0)()
                                if bg_work and j >= 1:
                                    bg_work.pop(0)()

                            def fin(blist=blist, nb=nb, op3=op3, ob=ob, hh=hh):
                                Rg = tile_rows(blist[0])
                                V(lambda e: e.tensor_copy(out=zc[:Rg, :nb], in_=op3[:Rg, :, 64]),
                                  [('ps', ob)], ['zc'])
                                V(lambda e: e.reciprocal(out=z2[:Rg, :nb], in_=zc[:Rg, :nb]), ['zc'], ['z2'])
                                for bi, b in enumerate(blist):
                                    V(lambda e, bi=bi: e.tensor_scalar(
                                        out=yv[:Rg, bi, :], in0=op3[:Rg, bi, 0:64], scalar1=z2[:Rg, bi:bi + 1],
                                        scalar2=None, op0=ALU.mult),
                                      [('ps', ob), 'z2'], ['cS'])
                                V(lambda e: e.tensor_tensor(out=sqo[:Rg, :nb, :], in0=yv[:Rg, :nb, :], in1=yv[:Rg, :nb, :], op=ALU.mult),
                                  ['cS'], ['cS'])
                                V(lambda e: e.reduce_sum(out=sso[:Rg, :nb], in_=sqo[:Rg, :nb, :], axis=AX.X), ['cS'], ['sso'])
                                A(lambda e: e.activation(out=lnt[:Rg, :nb], in_=sso[:Rg, :nb], func=AF.Ln, scale=1.0 / 64,
                                                         bias=epst[:Rg, 0:1]), ['sso', 'epst'], ['lnt'])
                                A(lambda e: e.activation(out=ro[:Rg, :nb], in_=lnt[:Rg, :nb], func=AF.Exp, scale=-0.5), ['lnt'], ['ro'])
                                for bi, b in enumerate(blist):
                                    V(lambda e, bi=bi, b=b: e.tensor_scalar(
                                        out=ytok[:Rg, b, hh * 64:(hh + 1) * 64], in0=yv[:Rg, bi, :], scalar1=ro[:Rg, bi:bi + 1],
                                        scalar2=None, op0=ALU.mult),
                                      ['cS', 'ro'], [('ytok', b)])
                            pending_fin.append(fin)
                        while work:
                            work.pop(0)()
                    if pend_pv is not None:
                        pend_pv(); pend_pv = None
                    if pending_fin:
                        pending_fin.pop()()
                    while bg_work:
                        bg_work.pop(0)()
                    gat = cst[:, C_GATTN + hp:C_GATTN + hp + 1]
                    for j in range(NT):
                        R = tile_rows(j); c0 = tile_col0(j)
                        b = rotA.next()
                        tpv = bank(b).bitcast(BF16)
                        P(lambda e, tpv=tpv, R=R, j=j: e.transpose(out=tpv[:, 0:R], in_=ytok[:R, j, :], identity=ident[:R, :R]),
                          [('ytok', j), 'ident'], [('ps', b)])
                        V(lambda e, tpv=tpv, R=R, c0=c0, s_=s_, gat=gat: e.tensor_scalar(out=yT[:, s_, c0:c0 + R], in0=tpv[:, 0:R], scalar1=gat,
                                                                                       scalar2=None, op0=ALU.mult),
                          [('ps', b), 'cst'], [('yT', s_)])
                if pair == 1:
                    wout_partial(wo_aps, wo_keys, tail=tail)
                else:
                    bg_work = wout_items(wo_aps, wo_keys)

        def final_store(s, do_norm=True):
            if do_norm:
                for j in range(1, NT):
                    A(lambda e, j=j: e.activation(out=junk[:, :], in_=h[:, j, :], func=AF.Square, accum_out=ss[:, j:j + 1]),
                      [('h', j)], [('pT', 0), ('pT', 1), 'ss'])
                act_fence(['ss'])
                A(lambda e: e.activation(out=lnv[:, :], in_=ss[:, :], func=AF.Ln, scale=1.0 / D, bias=epst[:, 0:1]),
                  ['ss', 'fence_out', 'epst'], ['lnv'])
                A(lambda e: e.activation(out=rstd[:, :], in_=lnv[:, :], func=AF.Exp, scale=-0.5), ['lnv'], ['rstd'])
            for j in range(1, NT):
                if do_norm:
                    V(lambda e, j=j: e.scalar_tensor_tensor(out=h[:, j, :], in0=h[:, j, :], scalar=rstd[:, j:j + 1],
                                                            in1=cst[:, C_GFIN:C_GFIN + D], op0=ALU.mult, op1=ALU.mult),
                      [('h', j), 'rstd', 'cst'], [('h', j)])
                S.add('sp', lambda e, j=j: e.dma_start(out=y_d[s, (j - 1) * 128:j * 128, :], in_=h[:, j, :]),
                      [('h', j)], [], dma=True)

        def load_group(s, g):
            for j in TOK_GROUPS[g]:
                if j == 0:
                    S.add('sp', lambda e: e.dma_start(out=h[0:NMETA, 0, :], in_=meta_d[:, :]), [], [('h', 0)], dma=True)
                else:
                    S.add('sp', lambda e, j=j, s=s: e.dma_start(out=h[:, j, :], in_=x_d[s, (j - 1) * 128:j * 128, :]),
                          [], [('h', j)], dma=True)

        def final_group(s, g):
            tl = [j for j in TOK_GROUPS[g] if j >= 1]
            if not tl:
                return
            j0, j1 = tl[0], tl[-1] + 1
            for j in tl:
                A(lambda e, j=j: e.activation(out=junk[:, :], in_=h[:, j, :], func=AF.Square, accum_out=ss[:, j:j + 1]),
                  [('h', j)], [('pT', 0), ('pT', 1), ('ss', g)])
            act_fence([('ss', g)])
            A(lambda e: e.activation(out=lnv[:, j0:j1], in_=ss[:, j0:j1], func=AF.Ln, scale=1.0 / D, bias=epst[:, 0:1]),
              [('ss', g), 'fence_out', 'epst'], [('lnv', g)])
            A(lambda e: e.activation(out=rstd[:, j0:j1], in_=lnv[:, j0:j1], func=AF.Exp, scale=-0.5), [('lnv', g)], [('rstd', g)])
            for j in tl:
                V(lambda e, j=j: e.scalar_tensor_tensor(out=h[:, j, :], in0=h[:, j, :], scalar=rstd[:, j:j + 1],
                                                        in1=cst[:, C_GFIN:C_GFIN + D], op0=ALU.mult, op1=ALU.mult),
                  [('h', j), ('rstd', g), 'cst'], [('h', j)])
                S.add('sp', lambda e, j=j: e.dma_start(out=y_d[s, (j - 1) * 128:j * 128, :], in_=h[:, j, :]),
                      [('h', j)], [], dma=True)

        dbg_mode = stop_after is not None
        for s in range(n_seq):
            if s == 0 or dbg_mode:
                for g in range(len(TOK_GROUPS)):
                    load_group(s, g)
            if dbg_mode:
                norm_to_xT(C_G1)
            if dbg_mode:
                ffn(f1_d)
                if stop_after == 'ffn1':
                    final_store(s, do_norm=False); continue
                norm_to_xT(C_GM)
                mixer()
                if stop_after == 'mix':
                    final_store(s, do_norm=False); continue
                norm_to_xT(C_G2)
                ffn(f2_d)
                final_store(s, do_norm=False); continue
            import os
            if os.environ.get('NO_TAILS'):
                if s > 0:
                    for g in range(len(TOK_GROUPS)):
                        load_group(s, g)
                norm_to_xT(C_G1)
                ffn(f1_d)
                norm_to_xT(C_GM)
                mixer()
                norm_to_xT(C_G2)
                ffn(f2_d)
                for g in range(len(TOK_GROUPS)):
                    final_group(s, g)
                continue
            norm_front(0, C_G1)()
            ffn(f1_d, tail=lambda g: norm_front(g, C_GM), head=lambda g: norm_front(g, C_G1)())
            mixer(tail=lambda g: norm_front(g, C_G2))
            def tail2(g, s=s):
                final_group(s, g)
                if s + 1 < n_seq:
                    load_group(s + 1, g)
                return None
            ffn(f2_d, tail=tail2)

        sem_ctx = {e: es.enter_context(nc.semaphore(f"tl_{e}")) for e in COMPUTE}
        dma_sems = {
            'sp': [es.enter_context(nc.semaphore(f"dsp{i}")) for i in range(24)],
            'pool': [es.enter_context(nc.semaphore(f"dpl{i}")) for i in range(8)],
        }
        streams, finals = S.lower(nc, sem_ctx, dma_sems)
        block = es.enter_context(nc.Block())

        def emit(engname, tail_waits=()):
            def f(e):
                for waits, op, sig in streams.get(engname, []):
                    for (sem, val) in waits:
                        e.wait_ge(sem, val)
                    ins = op.fn(e)
                    if sig is not None:
                        ins.then_inc(sig[0], sig[1])
                for (sem, val) in tail_waits:
                    e.wait_ge(sem, val)
            return f

        block.sync(emit('sp', finals.get('sp', [])))
        block.gpsimd(emit('pool', finals.get('pool', [])))
        block.tensor(emit('pe'))
        block.scalar(emit('act'))
        block.vector(emit('dve'))
    return nc


_PROG = {}


def kernel(**inputs):
    inp = {k: np.asarray(v) for k, v in inputs.items()}
    x = np.ascontiguousarray(inp['x'], dtype=np.float32)
    cst = make_consts(inp)
    f1 = ffn_layout(inp['ffn1_w_gu'].astype(np.float32), inp['ffn1_w_down'].astype(np.float32))
    f2 = ffn_layout(inp['ffn2_w_gu'].astype(np.float32), inp['ffn2_w_down'].astype(np.float32))
    mx = mix_layout(inp['w_in'].astype(np.float32), inp['w_out'].astype(np.float32))
    meta = np.ascontiguousarray(inp['meta_tokens'], dtype=np.float32)
    if 'nc' not in _PROG:
        _PROG['nc'] = build_program(SEQ_PER_CORE)
    nc = _PROG['nc']
    in_maps = []
    for c in range(N_CORES):
        in_maps.append({"x": x[c * SEQ_PER_CORE:(c + 1) * SEQ_PER_CORE], "meta": meta, "cst": cst,
                        "ffn1": f1, "ffn2": f2, "mixw": mx})
    res = run_bass_kernel_spmd(nc, in_maps, core_ids=list(range(N_CORES)))
    out = np.concatenate([np.asarray(r["y"]) for r in res.results], axis=0)
    return out.astype(np.float32)
```
